# Optimizing a Trainium2 kernel written in Bass

```python
import math
import jax, jax.numpy as jnp
from jax import lax
import numpy as np

D_MODEL = 1024
BATCH = 32
SEQ = 2048
DEPTH = 2

HEAD_DIM = 64
BLOCK_Q = 128
RMS_EPS = 1e-6
NEG_INF = -1e30
TINY = 1e-30

FOX_HEADS = 4
FORGET_BIAS_INIT = 3.0
DIL_WINDOWS = (128, 512, 2048)
DIL_RATES = (1, 4, 16)
DIL_HEADS_PER_GROUP = 2
DIL_HEADS = DIL_HEADS_PER_GROUP * len(DIL_WINDOWS)
NSA_HEADS = 6
NSA_KV_GROUPS = 2
NSA_CMP_BLOCK = 32
NSA_CMP_STRIDE = 16
NSA_CMP_HIDDEN = 256
NSA_SEL_BLOCK = 64
NSA_SEL_TOPK = 8
NSA_WINDOW = 512
FORCE_SCORE = 1e6
ALIBI_HEADS = DIL_HEADS + NSA_HEADS
D_FF = 2816
CONV_WIDTH = 3

FOX_W = FOX_HEADS * HEAD_DIM
DIL_W = DIL_HEADS * HEAD_DIM
NSA_W = NSA_HEADS * HEAD_DIM
NSA_KV_W = NSA_KV_GROUPS * HEAD_DIM
IN_SPLITS = (FOX_W, FOX_W, FOX_W, FOX_HEADS,
             DIL_W, DIL_W, DIL_W,
             NSA_W, NSA_KV_W, NSA_KV_W, NSA_KV_W, NSA_KV_W, NSA_KV_W, NSA_KV_W, NSA_HEADS * 3,
             3 * D_MODEL)
D_IN = sum(IN_SPLITS)

kernel_name = "hybrid_fox_dilated_nsa_convffn"


def rms_norm(x, g):
    xf = x.astype(jnp.float32)
    y = xf * lax.rsqrt(jnp.mean(xf * xf, axis=-1, keepdims=True) + RMS_EPS)
    return (y * g.astype(jnp.float32)).astype(x.dtype)


def alibi_slopes(n):
    return jnp.exp2(-8.0 * jnp.arange(1, n + 1, dtype=jnp.float32) / n)


def masked_softmax(s, mask):
    s = jnp.where(mask, s, NEG_INF)
    m = jnp.max(s, axis=-1, keepdims=True)
    e = jnp.where(mask, jnp.exp(s - m), 0.0)
    l = jnp.maximum(jnp.sum(e, axis=-1, keepdims=True), TINY)
    return e / l, (m + jnp.log(l))[..., 0]


def fox_attention(q, k, v, f_logit):
    B_, S, H, dh = q.shape
    nb = S // BLOCK_Q
    scale = dh ** -0.5
    c = jnp.cumsum(jax.nn.log_sigmoid(f_logit.astype(jnp.float32)), axis=1)
    c = jnp.transpose(c, (0, 2, 1))
    q_blocks = q.reshape(B_, nb, BLOCK_Q, H, dh).transpose(1, 0, 2, 3, 4)
    c_blocks = c.reshape(B_, H, nb, BLOCK_Q).transpose(2, 0, 1, 3)
    kpos = jnp.arange(S)

    def body(args):
        qb, cb, i = args
        qpos = i * BLOCK_Q + jnp.arange(BLOCK_Q)
        s = jnp.einsum('bqhd,bkhd->bhqk', qb, k).astype(jnp.float32) * scale
        s = s + cb[..., :, None] - c[:, :, None, :]
        p, _ = masked_softmax(s, kpos[None, :] <= qpos[:, None])
        return jnp.einsum('bhqk,bkhd->bqhd', p.astype(v.dtype), v)

    out = lax.map(body, (q_blocks, c_blocks, jnp.arange(nb)))
    return out.transpose(1, 0, 2, 3, 4).reshape(B_, S, H * dh)


def dilated_attention(q, k, v, slopes):
    B_, S, H, dh = q.shape
    P = DIL_HEADS_PER_GROUP
    nb = S // BLOCK_Q
    scale = dh ** -0.5
    q_blocks = q.reshape(B_, nb, BLOCK_Q, H, dh).transpose(1, 0, 2, 3, 4)

    def body(args):
        qb, i = args
        qpos = i * BLOCK_Q + jnp.arange(BLOCK_Q)
        outs, lses = [], []
        for g, (w, r) in enumerate(zip(DIL_WINDOWS, DIL_RATES)):
            hs = slice(g * P, (g + 1) * P)
            offs = r * jnp.arange(w // r + 1)
            kidx = qpos[:, None] - offs[None, :]
            valid = kidx >= 0
            kidx = jnp.maximum(kidx, 0)
            kg = k[:, :, hs][:, kidx]
            vg = v[:, :, hs][:, kidx]
            s = jnp.einsum('bqhd,bqkhd->bhqk', qb[:, :, hs], kg).astype(jnp.float32) * scale
            s = s - slopes[hs][None, :, None, None] * offs.astype(jnp.float32)[None, None, None, :]
            p, lse = masked_softmax(s, valid[None, None])
            outs.append(jnp.einsum('bhqk,bqkhd->bqhd', p.astype(vg.dtype), vg))
            lses.append(lse)
        alpha = jax.nn.softmax(jnp.stack(lses, 0), axis=0)
        o = jnp.stack(outs, 0) * alpha.transpose(0, 1, 3, 2)[..., None].astype(qb.dtype)
        return o.transpose(1, 2, 0, 3, 4).reshape(B_, BLOCK_Q, H * dh)

    out = lax.map(body, (q_blocks, jnp.arange(nb)))
    return out.transpose(1, 0, 2, 3).reshape(B_, S, H * dh)


def compress_blocks(kv, pe, w1, w2):
    B_, S, G, dh = kv.shape
    n_cmp = (S - NSA_CMP_BLOCK) // NSA_CMP_STRIDE + 1
    idx = jnp.arange(n_cmp)[:, None] * NSA_CMP_STRIDE + jnp.arange(NSA_CMP_BLOCK)[None, :]
    blocks = kv[:, idx] + pe[None, None, :, None, :]
    flat = blocks.transpose(0, 1, 3, 2, 4).reshape(B_, n_cmp, G, NSA_CMP_BLOCK * dh)
    return jax.nn.gelu(flat @ w1) @ w2


def nsa_attention(q, k_cmp, v_cmp, k_sel, v_sel, k_win, v_win, gate_logits, slopes,
                  cmp_pe, k_w1, k_w2, v_w1, v_w2):
    B_, S, H, dh = q.shape
    G = NSA_KV_GROUPS
    R = H // G
    nb = S // BLOCK_Q
    scale = dh ** -0.5
    kc = compress_blocks(k_cmp, cmp_pe, k_w1, k_w2)
    vc = compress_blocks(v_cmp, cmp_pe, v_w1, v_w2)
    n_cmp = kc.shape[1]
    cmp_end = jnp.arange(n_cmp) * NSA_CMP_STRIDE + NSA_CMP_BLOCK - 1
    cmp_start = cmp_end - (NSA_CMP_BLOCK - 1)
    n_sel = S // NSA_SEL_BLOCK
    sel_idx = jnp.arange(n_sel)
    sel_start = sel_idx * NSA_SEL_BLOCK
    overlap = ((cmp_start[:, None] < sel_start[None, :] + NSA_SEL_BLOCK)
               & (cmp_end[:, None] >= sel_start[None, :])).astype(jnp.float32)
    top_k = min(NSA_SEL_TOPK, n_sel)
    n_tok = top_k * NSA_SEL_BLOCK
    slope = slopes.reshape(G, R)[None, :, :, None, None]
    ks_t = k_sel.transpose(0, 2, 1, 3)
    vs_t = v_sel.transpose(0, 2, 1, 3)
    pad = ((0, 0), (NSA_WINDOW, 0), (0, 0), (0, 0))
    kw_pad = jnp.pad(k_win, pad)
    vw_pad = jnp.pad(v_win, pad)
    gather = jax.vmap(jax.vmap(lambda t, ii: t[ii]))
    q_blocks = q.reshape(B_, nb, BLOCK_Q, G, R, dh).transpose(1, 0, 2, 3, 4, 5)
    g_blocks = jax.nn.sigmoid(gate_logits).reshape(B_, nb, BLOCK_Q, G, R, 3).transpose(1, 0, 2, 3, 4, 5)

    def body(args):
        qb, gb, i = args
        q0 = i * BLOCK_Q
        qpos = q0 + jnp.arange(BLOCK_Q)
        s = jnp.einsum('bqgrd,bcgd->bgrqc', qb, kc).astype(jnp.float32) * scale
        s = s - slope * (qpos[:, None] - cmp_end[None, :]).astype(jnp.float32)
        p_c, _ = masked_softmax(s, cmp_end[None, :] <= qpos[:, None])
        o_c = jnp.einsum('bgrqc,bcgd->bqgrd', p_c.astype(vc.dtype), vc)
        imp = jnp.einsum('bgrqc,cj->bgqj', p_c, overlap)
        qblk = qpos // NSA_SEL_BLOCK
        causal = sel_start[None, :] <= qpos[:, None]
        forced = ((sel_idx[None, :] == 0) | (sel_idx[None, :] == qblk[:, None])
                  | (sel_idx[None, :] == qblk[:, None] - 1))
        score = jnp.where(causal & forced, FORCE_SCORE, jnp.where(causal, imp, -1.0))
        _, blk = lax.top_k(score, top_k)
        blk_ok = blk * NSA_SEL_BLOCK <= qpos[None, None, :, None]
        tok = (blk[..., None] * NSA_SEL_BLOCK + jnp.arange(NSA_SEL_BLOCK)).reshape(B_, G, BLOCK_Q, n_tok)
        tok_ok = jnp.repeat(blk_ok, NSA_SEL_BLOCK, axis=-1) & (tok <= qpos[None, None, :, None])
        kg = gather(ks_t, tok)
        vg = gather(vs_t, tok)
        s = jnp.einsum('bqgrd,bgqtd->bgrqt', qb, kg).astype(jnp.float32) * scale
        s = s - slope * (qpos[None, None, :, None] - tok)[:, :, None].astype(jnp.float32)
        p_s, _ = masked_softmax(s, tok_ok[:, :, None])
        o_s = jnp.einsum('bgrqt,bgqtd->bqgrd', p_s.astype(vg.dtype), vg)
        kw = lax.dynamic_slice_in_dim(kw_pad, q0, NSA_WINDOW + BLOCK_Q, axis=1)
        vw = lax.dynamic_slice_in_dim(vw_pad, q0, NSA_WINDOW + BLOCK_Q, axis=1)
        kpos = q0 - NSA_WINDOW + jnp.arange(NSA_WINDOW + BLOCK_Q)
        diff = qpos[:, None] - kpos[None, :]
        mask_w = (kpos[None, :] >= 0) & (diff >= 0) & (diff < NSA_WINDOW)
        s = jnp.einsum('bqgrd,bkgd->bgrqk', qb, kw).astype(jnp.float32) * scale
        s = s - slope * diff.astype(jnp.float32)
        p_w, _ = masked_softmax(s, mask_w)
        o_w = jnp.einsum('bgrqk,bkgd->bqgrd', p_w.astype(vw.dtype), vw)
        gb = gb.astype(qb.dtype)
        o = gb[..., 0:1] * o_c + gb[..., 1:2] * o_s + gb[..., 2:3] * o_w
        return o.reshape(B_, BLOCK_Q, H * dh)

    out = lax.map(body, (q_blocks, g_blocks, jnp.arange(nb)))
    return out.transpose(1, 0, 2, 3).reshape(B_, S, H * dh)


def conv_ffn(h, w_up, conv_w, conv_b, w_down):
    a, b = jnp.split(h @ w_up, 2, axis=-1)
    a = lax.conv_general_dilated(a, conv_w.reshape(CONV_WIDTH, 1, D_FF), window_strides=(1,),
                                 padding=[(CONV_WIDTH - 1, 0)],
                                 dimension_numbers=('NWC', 'WIO', 'NWC'),
                                 feature_group_count=D_FF) + conv_b
    return (jax.nn.gelu(a) * b) @ w_down


def setup_inputs(seed: int = 0) -> dict:
    key = jax.random.key(seed)
    ks = jax.random.split(key, 20)
    f32 = jnp.float32

    def nrm(k, shape, scale):
        return jax.random.normal(k, shape, f32) * scale

    L = DEPTH
    fo = 3 * FOX_W
    b_in = nrm(ks[3], (L, D_IN), 0.02).at[:, fo:fo + FOX_HEADS].add(FORGET_BIAS_INIT)
    return {
        "x": nrm(ks[0], (BATCH, SEQ, D_MODEL), 1.0),
        "norm1_g": 1.0 + nrm(ks[1], (L, D_MODEL), 0.02),
        "w_in": nrm(ks[2], (L, D_MODEL, D_IN), D_MODEL ** -0.5),
        "b_in": b_in,
        "cmp_pe": nrm(ks[4], (L, NSA_CMP_BLOCK, HEAD_DIM), 0.02),
        "cmp_k_w1": nrm(ks[5], (L, NSA_CMP_BLOCK * HEAD_DIM, NSA_CMP_HIDDEN), (NSA_CMP_BLOCK * HEAD_DIM) ** -0.5),
        "cmp_k_w2": nrm(ks[6], (L, NSA_CMP_HIDDEN, HEAD_DIM), NSA_CMP_HIDDEN ** -0.5),
        "cmp_v_w1": nrm(ks[7], (L, NSA_CMP_BLOCK * HEAD_DIM, NSA_CMP_HIDDEN), (NSA_CMP_BLOCK * HEAD_DIM) ** -0.5),
        "cmp_v_w2": nrm(ks[8], (L, NSA_CMP_HIDDEN, HEAD_DIM), NSA_CMP_HIDDEN ** -0.5),
        "w_pa": nrm(ks[9], (L, FOX_W, D_MODEL), FOX_W ** -0.5),
        "w_pb": nrm(ks[10], (L, DIL_W, D_MODEL), DIL_W ** -0.5),
        "w_pc": nrm(ks[11], (L, NSA_W, D_MODEL), NSA_W ** -0.5),
        "w_o": nrm(ks[12], (L, D_MODEL, D_MODEL), D_MODEL ** -0.5),
        "norm2_g": 1.0 + nrm(ks[13], (L, D_MODEL), 0.02),
        "w_up": nrm(ks[14], (L, D_MODEL, 2 * D_FF), D_MODEL ** -0.5),
        "conv_w": nrm(ks[15], (L, CONV_WIDTH, D_FF), CONV_WIDTH ** -0.5),
        "conv_b": nrm(ks[16], (L, D_FF), 0.02),
        "w_down": nrm(ks[17], (L, D_FF, D_MODEL), D_FF ** -0.5),
        "final_g": 1.0 + nrm(ks[18], (D_MODEL,), 0.02),
    }


def reference(x, norm1_g, w_in, b_in, cmp_pe, cmp_k_w1, cmp_k_w2, cmp_v_w1, cmp_v_w2,
              w_pa, w_pb, w_pc, w_o, norm2_g, w_up, conv_w, conv_b, w_down, final_g):
    B_, S, D = x.shape
    dh = HEAD_DIM
    G = NSA_KV_GROUPS
    slopes = alibi_slopes(ALIBI_HEADS)
    dil_slopes = slopes[0::2]
    nsa_slopes = slopes[1::2]
    split_at = np.cumsum(IN_SPLITS)[:-1].tolist()
    for l in range(DEPTH):
        h = rms_norm(x, norm1_g[l])
        proj = h @ w_in[l] + b_in[l]
        (qa, ka, va, fa, qb, kb, vb, qc, kc_, vc_, ksl, vsl, kwn, vwn, gc, gates) = jnp.split(proj, split_at, axis=-1)
        y_a = fox_attention(qa.reshape(B_, S, FOX_HEADS, dh), ka.reshape(B_, S, FOX_HEADS, dh),
                            va.reshape(B_, S, FOX_HEADS, dh), fa)
        y_b = dilated_attention(qb.reshape(B_, S, DIL_HEADS, dh), kb.reshape(B_, S, DIL_HEADS, dh),
                                vb.reshape(B_, S, DIL_HEADS, dh), dil_slopes)
        y_c = nsa_attention(qc.reshape(B_, S, NSA_HEADS, dh),
                            kc_.reshape(B_, S, G, dh), vc_.reshape(B_, S, G, dh),
                            ksl.reshape(B_, S, G, dh), vsl.reshape(B_, S, G, dh),
                            kwn.reshape(B_, S, G, dh), vwn.reshape(B_, S, G, dh),
                            gc.reshape(B_, S, NSA_HEADS, 3), nsa_slopes,
                            cmp_pe[l], cmp_k_w1[l], cmp_k_w2[l], cmp_v_w1[l], cmp_v_w2[l])
        g = jax.nn.sigmoid(gates).reshape(B_, S, 3, D)
        merged = (g[:, :, 0] * (y_a @ w_pa[l]) + g[:, :, 1] * (y_b @ w_pb[l])
                  + g[:, :, 2] * (y_c @ w_pc[l]))
        x = x + merged @ w_o[l]
        x = x + conv_ffn(rms_norm(x, norm2_g[l]), w_up[l], conv_w[l], conv_b[l], w_down[l])
    return rms_norm(x, final_g)
```

```python
import math
from contextlib import ExitStack
import numpy as np
import ml_dtypes
import concourse.bass as bass
import concourse.mybir as mybir
from concourse.bass_utils import run_bass_kernel_spmd

F32 = mybir.dt.float32
BF16 = mybir.dt.bfloat16
AF = mybir.ActivationFunctionType
ALU = mybir.AluOpType
NPBF = ml_dtypes.bfloat16

S = 2048
D = 1024
NT = 16
DC = 8
NTG = 4
DIN = 6166
DFF = 2816
FC = 22
NEG = -30000.0
C_QA, C_KA, C_VA, C_FA = 0, 256, 512, 768
C_QB, C_KB, C_VB = 772, 1156, 1540
C_QC, C_KC, C_VC, C_KSL, C_VSL, C_KWN, C_VWN, C_GC, C_GATES = 1924, 2308, 2436, 2564, 2692, 2820, 2948, 3076, 3094
DIL_R = (1, 4, 16)
ZERO_INIT = False


class Sched:
    def __init__(self, nc, es, n_dma=32):
        self.nc = nc
        self.engs = {"pe": "tensor", "act": "scalar", "dve": "vector", "pool": "gpsimd", "sp": "sync"}
        self.ops = {e: [] for e in self.engs}
        self.sem = {e: es.enter_context(nc.semaphore("s_" + e)) for e in self.engs}
        self.dsem = [es.enter_context(nc.semaphore("d%d" % i)) for i in range(n_dma)]
        self.dcount = [0] * n_dma
        self.dnext = 0
        self.lastw = {}
        self.readers = {}
        self.known = {e: {} for e in self.engs}
        self.targets = {e: set() for e in self.engs}

    def _deps(self, reads, writes):
        toks = []
        for k in reads:
            toks.extend(self.lastw.get(k, {}).values())
        for k in writes:
            toks.extend(self.lastw.get(k, {}).values())
            toks.extend(self.readers.get(k, {}).values())
        return toks

    def _waits(self, eng, toks, same_ok):
        w = {}
        for (s, i) in toks:
            if same_ok and s == eng:
                continue
            if self.known[eng].get(s, -1) >= i:
                continue
            if w.get(s, -1) < i:
                w[s] = i
        for s, i in w.items():
            self.known[eng][s] = i
            if not isinstance(s, int):
                self.targets[s].add(i)
        return list(w.items())

    def _commit(self, tok, reads, writes):
        for k in reads:
            self.readers.setdefault(k, {})[tok[0]] = tok
        for k in writes:
            self.lastw.setdefault(k, {})[tok[0]] = tok
            self.readers[k] = {}

    def op(self, eng, fn, reads=(), writes=(), relaxed=False):
        toks = self._deps(reads, writes)
        waits = self._waits(eng, toks, eng == "pe" or relaxed)
        idx = len(self.ops[eng])
        self.ops[eng].append(("c", fn, waits, None))
        self._commit((eng, idx), reads, writes)

    def dma(self, q, fn, reads=(), writes=()):
        k = self.dnext
        self.dnext = (k + 1) % len(self.dsem)
        toks = self._deps(reads, writes)
        if self.dcount[k] > 0:
            toks.append((k, self.dcount[k]))
        waits = self._waits(q, toks, False)
        self.dcount[k] += 1
        self.ops[q].append(("d", fn, waits, k))
        self._commit((k, self.dcount[k]), reads, writes)

    def alias(self, new, olds):
        rd = dict(self.readers.get(new, {}))
        for o_ in olds:
            toks = list(self.readers.get(o_, {}).values()) + list(self.lastw.get(o_, {}).values())
            for (s, i) in toks:
                if s not in rd or rd[s][1] < i:
                    rd[s] = (s, i)
        self.readers[new] = rd

    def final_wait(self, q, keys):
        toks = self._deps(keys, ())
        waits = self._waits(q, toks, False)
        self.ops[q].append(("w", None, waits, None))

    def emit(self):
        rank = {}
        for e in self.engs:
            rank[e] = {i: r + 1 for r, i in enumerate(sorted(self.targets[e]))}
        with self.nc.Block() as block:
            for e, attr in self.engs.items():
                ops = self.ops[e]
                if not ops:
                    continue

                def body(eng, e=e, ops=ops):
                    for idx, (kind, fn, waits, k) in enumerate(ops):
                        for (s, i) in waits:
                            if isinstance(s, int):
                                eng.wait_ge(self.dsem[s], 16 * i)
                            else:
                                eng.wait_ge(self.sem[s], rank[s][i])
                        if kind == "w":
                            continue
                        ins = fn(eng)
                        if kind == "d":
                            ins.then_inc(self.dsem[k], 16)
                        elif idx in rank[e]:
                            ins.then_inc(self.sem[e], 1)

                getattr(block, attr)(body)


def _bf16_limbs(v, n):
    v = np.asarray(v, np.float64)
    out = []
    r = v.copy()
    for _ in range(n):
        l = r.astype(np.float32).astype(NPBF)
        out.append(l)
        r = r - l.astype(np.float64)
    return out


def _alibi_slopes():
    n = 12
    s = np.exp2(-8.0 * np.arange(1, n + 1, dtype=np.float32) / n).astype(np.float32)
    return s[0::2].astype(np.float64), s[1::2].astype(np.float64)


def _make_consts():
    c = {}
    c["ident"] = np.eye(128, dtype=np.float32).astype(NPBF)
    k = np.arange(128)[:, None]
    q = np.arange(128)[None, :]
    mA = np.where(k <= q, 0.0, NEG)
    mB = np.where(k >= q, 0.0, NEG)
    mBs = np.where(k > q, 0.0, NEG)
    c["maskab"] = np.concatenate([mA, mB, mBs], 1).astype(np.float32).astype(NPBF)
    cc = np.arange(128)[:, None]
    qq = np.arange(S)[None, :]
    cm = np.where((16 * cc + 31 <= qq) & (cc < 127), 0.0, NEG)
    c["cmask"] = cm.astype(np.float32).astype(NPBF)
    dil_s, nsa_s = _alibi_slopes()
    qtab = np.zeros((12, 7, S), NPBF)
    ktab = np.zeros((5, 7, S), NPBF)

    def krows(pos):
        pos = np.asarray(pos, np.int64)
        pH = (pos // 128) * 128
        pL = pos % 128
        one = np.ones_like(pos, np.float64)
        return np.stack([one, one, one, pH, pH, pL, pL]).astype(np.float32).astype(NPBF)

    for g, r in enumerate(DIL_R):
        Lc = S // r
        a = np.tile(np.arange(Lc), r)
        ktab[g] = krows(a)
        for p in range(2):
            h = 2 * g + p
            sl = dil_s[h] * r
            hi, lo = _bf16_limbs(np.full(S, sl), 2)
            L = _bf16_limbs(-sl * a.astype(np.float64), 3)
            qtab[h] = np.stack([L[0], L[1], L[2], hi, lo, hi, lo])
    pos = np.arange(S)
    ktab[3] = krows(pos)
    ce = np.zeros(S, np.int64)
    ce[:127] = 16 * np.arange(127) + 31
    ktab[4] = krows(ce)
    for h in range(6):
        sl = nsa_s[h]
        hi, lo = _bf16_limbs(np.full(S, sl), 2)
        L = _bf16_limbs(-sl * pos.astype(np.float64), 3)
        qtab[6 + h] = np.stack([L[0], L[1], L[2], hi, lo, hi, lo])
    c["qtab"] = qtab
    c["ktab"] = ktab
    j = np.arange(32)[:, None]
    kk = np.arange(S)[None, :]
    c["ksel"] = (kk // 64 == j).astype(np.float32).astype(NPBF)
    qpos = (np.arange(16)[None, :, None] * 128 + np.arange(128)[:, None, None])
    jj = np.arange(32)[None, None, :]
    causal = (jj * 64 <= qpos)
    qblk = qpos // 64
    forced = (jj == 0) | (jj == qblk) | (jj == qblk - 1)
    m1 = (causal & ~forced).astype(np.float32)
    m2 = np.where(causal & forced, 1e6, np.where(causal, 0.0, -1.0)).astype(np.float32)
    c["m1"] = m1.reshape(128, 512)
    c["m2"] = m2.reshape(128, 512)
    cidx = np.arange(128)[:, None]
    cstart = 16 * cidx
    cend = cstart + 31
    sst = np.arange(32)[None, :] * 64
    ov = ((cstart < sst + 64) & (cend >= sst) & (cidx < 127)).astype(np.float32)
    ov33 = np.concatenate([ov, (cidx < 127).astype(np.float32)], 1)
    c["ov33"] = ov33.astype(NPBF)
    sg = np.zeros((9, 9, 128), np.float32)
    for k_ in range(9):
        sg[k_, k_, :] = 1.0
    c["selg"] = sg.astype(NPBF)
    return c


CONST_SPECS = [
    ("ident", [128, 128], BF16), ("maskab", [128, 384], BF16), ("cmask", [128, S], BF16),
    ("qtab", [12, 7, S], BF16), ("ktab", [5, 7, S], BF16), ("ksel", [32, S], BF16),
    ("m1", [128, 512], F32), ("m2", [128, 512], F32), ("ov33", [128, 33], BF16), ("selg", [9, 9, 128], BF16),
]

WEIGHTS = [
    ("w_in", [D, DIN]), ("w_pa", [256, D]), ("w_pb", [384, D]), ("w_pc", [384, D]), ("w_o", [D, D]),
    ("w_up", [D, 2 * DFF]), ("w_down", [DFF, D]),
    ("cmp_k_w1", [2048, 256]), ("cmp_k_w2", [256, 64]), ("cmp_v_w1", [2048, 256]), ("cmp_v_w2", [256, 64]),
]

def _bias_blocks():
    bl = []
    for p in range(2):
        bl.append(("qa_p%d" % p, [(0, C_QA + 128 * p, 128)]))
        bl.append(("ka_p%d" % p, [(0, C_KA + 128 * p, 128)]))
    bl.append(("fa", [(0, C_FA, 4)]))
    for g in range(3):
        bl.append(("qb_g%d" % g, [(0, C_QB + 128 * g, 128)]))
        bl.append(("kb_g%d" % g, [(0, C_KB + 128 * g, 128)]))
        bl.append(("vb_g%d" % g, [(0, C_VB + 128 * g, 128)]))
    for g in range(2):
        bl.append(("n0_g%d" % g, [(0, C_QC + 192 * g, 128)]))
        bl.append(("n1_g%d" % g, [(0, C_QC + 192 * g + 128, 64), (64, C_KSL + 64 * g, 64)]))
        bl.append(("n2_g%d" % g, [(0, C_KWN + 64 * g, 64), (64, C_KC + 64 * g, 64)]))
        bl.append(("n3_g%d" % g, [(0, C_VC + 64 * g, 64)]))
        bl.append(("gc%d" % g, [(64, C_GC + 9 * g, 9)]))
    for i in range(24):
        bl.append(("gt%d" % i, [(0, C_GATES + 128 * i, 128)]))
    return bl


BIAS_BLOCKS = _bias_blocks()
BIDX = {n: i for i, (n, _) in enumerate(BIAS_BLOCKS)}
NB = len(BIAS_BLOCKS)
VROW_COLS = (list(range(C_VA, C_VA + 256)) + list(range(C_VSL, C_VSL + 64)) + list(range(C_VWN, C_VWN + 64))
             + list(range(C_VSL + 64, C_VSL + 128)) + list(range(C_VWN + 64, C_VWN + 128)))


def build_program(nseq, depth, enable=("fox", "dil", "nsa"), debug=None, stop_after=None):
    nc = bass.Bass("TRN2", target_bir_lowering=False)
    es = ExitStack()
    dr = {}

    def din(name, shape, dt):
        dr[name] = nc.dram_tensor(name, list(shape), dt, kind="ExternalInput").ap()
        return dr[name]

    x_in = din("x", [nseq, S, D], F32)
    for n, shp in WEIGHTS:
        din(n, [depth] + shp, F32)
    din("ngb", [2 * depth + 1, 128, D], F32)
    din("ngt", [2 * depth + 1, 128, D], F32)
    din("bcol", [depth, 128, NB], F32)
    din("vrow", [depth, 1, 512], F32)
    din("cw", [depth, 128, FC * 4], F32)
    din("pet", [depth, 64, 32], F32)
    for n, shp, dt in CONST_SPECS:
        din(n, shp, dt)
    out = nc.dram_tensor("out", [nseq, S, D], F32, kind="ExternalOutput").ap()
    wb = {n: nc.dram_tensor("wb_" + n, [depth] + shp, BF16, kind="Internal").ap() for n, shp in WEIGHTS}
    skind = "ExternalOutput" if debug else "Internal"
    wg = nc.dram_tensor("wg", [depth, 8, 128, 3072], BF16, kind="Internal").ap()
    wu = nc.dram_tensor("wu", [depth, 11, 128, 4096], BF16, kind="Internal").ap()
    wd = nc.dram_tensor("wd", [depth, 4, 128, 5632], BF16, kind="Internal").ap()
    xres = nc.dram_tensor("xres", [S, D], F32, kind=skind).ap()
    hTd = nc.dram_tensor("hTd", [128, DC, S], BF16, kind=skind).ap()
    ytd = nc.dram_tensor("ytd", [128, 8, S], BF16, kind=skind).ap()
    dbg = None

    sc = Sched(nc, es)

    def sb(name, shape, dt):
        return es.enter_context(nc.sbuf_tensor(name, list(shape), dt))

    def ps(name, shape, dt=F32):
        return es.enter_context(nc.psum_tensor(name, list(shape), dt))

    NAR = 8192 + 16384 + 16384 + 6144 + 9600 + 5632 + 5632
    AR = sb("arena", [128, NAR], BF16)
    o = 0
    HS = [AR[:, o + i * 4096:o + (i + 1) * 4096].rearrange("p (c t) -> p c t", c=DC) for i in range(2)]
    o += 8192
    YT = AR[:, o:o + 16384].rearrange("p (c t) -> p c t", c=8)
    o += 16384
    QK = [AR[:, o + i * 2048:o + (i + 1) * 2048] for i in range(8)]
    QKALL = AR[:, o:o + 16384]
    o += 16384
    VV = AR[:, o:o + 6144].rearrange("p (t pr x) -> p t pr x", t=16, pr=2)
    o += 6144
    WM = AR[:, o:o + 9600]
    o += 9600
    W01 = [AR[:, o + i * 5632:o + (i + 1) * 5632] for i in range(2)]
    W01ALL = AR[:, o:o + 11264]
    gs_ctr = [0]
    o += 11264
    FA = sb("f32a", [128, 3072], F32)
    XT = [FA[:, i * 1024:(i + 1) * 1024] for i in range(3)]
    GBC = sb("gbc", [128, D], F32)
    HN = [sb("hn%d" % i, [128, D], BF16) for i in range(2)]
    SS = sb("ss", [128, 32], F32)
    IDENT = sb("ident_s", [128, 128], BF16)
    MASKAB = sb("maskab_s", [128, 384], BF16)
    CMASK = sb("cmask_s", [128, S], BF16)
    M1 = sb("m1_s", [128, 512], F32)
    M2 = sb("m2_s", [128, 512], F32)
    OV33 = sb("ov33_s", [128, 33], BF16)
    BCOL = sb("bcol_s", [128, NB], F32)
    VROW = sb("vrow_s", [1, 512], BF16)
    ONESB = sb("onesb", [128, 128], BF16)
    ZEROB = sb("zerob", [128, 128], BF16)
    PT = [sb("pt%d" % i, [128, 512], BF16) for i in range(3)]
    RB = [sb("rb%d" % i, [128, 512], F32) for i in range(2)]
    T1 = sb("t1", [128, 512], F32)
    FL = T1[0:4, :]
    LS = sb("ls", [128, S], F32)
    CSP = sb("csp", [4, 1], F32)
    CK = sb("ck", [4, 2, 512], BF16)
    CQ = sb("cq", [4, 2, 512], BF16)
    TMPB = [sb("tmpb%d" % i, [128, 512], BF16) for i in range(3)]
    KCT = sb("kct", [128, 128], BF16)
    VC = sb("vc", [128, 192], BF16)
    HG = [sb("hg%d" % i, [128, 2, 128], BF16) for i in range(2)]
    HB = sb("hb", [128, 4], F32)
    W2T = sb("w2t", [128, 2, 2, 64], BF16)
    PETF = sb("petf", [64, 32], F32)
    PETB = sb("petb", [64, 32], BF16)
    IMPS = sb("imps", [128, 512], F32)
    SCR = sb("scr", [128, 512], F32)
    CS = SCR[0:4, :]
    MX = sb("mx", [128, 16, 8], F32)
    RI = sb("ri", [128, 4], F32)
    NSEL = sb("nsel", [128, 512], BF16)
    CW = sb("cw_s", [128, FC * 4], F32)
    AH = sb("ah", [128, FC, 2], F32)
    GSB = sb("gsb", [128, S], BF16)
    SELG = sb("selg_s", [128, 9, 128], BF16)

    PS_S = [ps("ps_s%d" % i, [128, 512]) for i in range(3)]
    PS_A = [ps("ps_a%d" % i, [128, 512]) for i in range(2)]
    PS_B = ps("ps_b", [128, 512])
    PS_P = [ps("ps_p%d" % i, [128, 512]) for i in range(2)]
    LNB = sb("lnb", [128, 1], F32)

    K = lambda *a: a

    def mm(out_, lhsT, rhs, start, stop, reads, writes):
        sc.op("pe", lambda e: e.matmul(out_, lhsT=lhsT, rhs=rhs, start=start, stop=stop, skip_group_check=True),
              reads, writes)

    def act(out_, in_, func, reads, writes, bias=None, scale=None, accum_out=None):
        kw = {}
        if bias is not None:
            kw["bias"] = bias
        if scale is not None:
            kw["scale"] = scale
        if accum_out is not None:
            kw["accum_out"] = accum_out
        sc.op("act", lambda e: e.activation(out=out_, in_=in_, func=func, **kw), reads, writes)

    def ts(eng, out_, in0, s1, s2, op0, op1, reads, writes):
        if op1 is None:
            sc.op(eng, lambda e: e.tensor_scalar(out=out_, in0=in0, scalar1=s1, scalar2=None, op0=op0), reads, writes)
        else:
            sc.op(eng, lambda e: e.tensor_scalar(out=out_, in0=in0, scalar1=s1, scalar2=s2, op0=op0, op1=op1),
                  reads, writes)

    def tt(eng, out_, in0, in1, op, reads, writes):
        sc.op(eng, lambda e: e.tensor_tensor(out=out_, in0=in0, in1=in1, op=op), reads, writes)

    def stt(out_, in0, scalar, in1, op0, op1, reads, writes, relaxed=False):
        sc.op("dve", lambda e: e.scalar_tensor_tensor(out=out_, in0=in0, scalar=scalar, in1=in1, op0=op0, op1=op1),
              reads, writes, relaxed=relaxed)

    def cp(eng, out_, in_, reads, writes):
        if eng == "act":
            sc.op("act", lambda e: e.copy(out=out_, in_=in_), reads, writes)
        else:
            sc.op(eng, lambda e: e.tensor_copy(out=out_, in_=in_), reads, writes)

    def dma(q, out_, in_, reads, writes):
        sc.dma(q, lambda e: e.dma_start(out=out_, in_=in_), reads, writes)

    def memset(eng, ap, val, writes):
        sc.op(eng, lambda e: e.memset(ap, val), (), writes)

    dma("sp", IDENT[:, :], dr["ident"][:, :], (), ["ident"])
    dma("sp", MASKAB[:, :], dr["maskab"][:, :], (), ["maskab"])
    dma("sp", CMASK[:, :], dr["cmask"][:, :], (), ["cmask"])
    dma("sp", M1[:, :], dr["m1"][:, :], (), ["m1"])
    dma("sp", M2[:, :], dr["m2"][:, :], (), ["m2"])
    dma("sp", OV33[:, :], dr["ov33"][:, :], (), ["ov33"])
    memset("pool", ONESB[:, :], 1.0, ["onesb"])
    memset("pool", ZEROB[:, :], 0.0, ["zerob"])
    memset("pool", LNB[:, :], 1e-18, ["lnb"])
    memset("pool", QKALL, 0.0, ["qk%d" % i for i in range(8)])
    memset("pool", VV[:, :, :, 64:128], 1.0, ["vv"])
    memset("pool", VC[:, :], 0.0, ["vc"])
    memset("pool", VC[:, 64:128], 1.0, ["vc"])
    memset("pool", SELG[:, :, :], 0.0, ["selg"])
    memset("pool", GSB[:, :], 0.0, ["gsb"])
    dma("sp", SELG[64:73, :, :], dr["selg"][:, :, :], (), ["selg"])
    memset("pool", KCT[:, :], 0.0, ["kct"])
    memset("pool", AH[:, :, :], 0.0, ["ah"])

    conv_order = ["w_in", "cmp_k_w1", "cmp_k_w2", "cmp_v_w1", "cmp_v_w2", "w_pa", "w_pb", "w_pc", "w_o", "w_up", "w_down"]
    for l in range(depth):
        for n in conv_order:
            shp = dict(WEIGHTS)[n]
            rows = shp[0]
            step = 512 if rows > 512 else rows
            for r0 in range(0, rows, step):
                r1 = min(rows, r0 + step)
                dma("pool", wb[n][l, r0:r1, :], dr[n][l, r0:r1, :], (), [K("wb", n, l)])

    def relayout(l):
        for blk in range(24):
            c0 = C_GATES + blk * 128
            b_, c_ = divmod(blk, 8)
            dma("sp", wg[l, c_][:, b_ * 1024:(b_ + 1) * 1024].rearrange("p (c n) -> p c n", c=DC), wb["w_in"][l, :, c0:c0 + 128].rearrange("(c p) n -> p c n", p=128),
                [K("wb", "w_in", l)], [K("wg", l)])
        for fp in range(FC // 2):
            for ab in range(2):
                c0 = ab * DFF + fp * 256
                dma("sp", wu[l, fp].rearrange("p (c b n) -> p c b n", c=DC, b=2)[:, :, ab, :],
                    wb["w_up"][l, :, c0:c0 + 256].rearrange("(c p) n -> p c n", p=128), [K("wb", "w_up", l)], [K("wu", l)])
        for qd in range(4):
            dma("sp", wd[l, qd].rearrange("p (c n) -> p c n", c=FC), wb["w_down"][l, :, qd * 256:(qd + 1) * 256].rearrange("(c p) n -> p c n", p=128),
                [K("wb", "w_down", l)], [K("wd", l)])

    def load_layer_small(l):
        dma("sp", BCOL[:, :], dr["bcol"][l], (), ["bcol"])
        dma("pool", VROW[:, :], dr["vrow"][l], (), ["vrow"])
        dma("sp", CW[:, :], dr["cw"][l], (), ["cw"])
        dma("sp", PETF[:, :], dr["pet"][l], (), ["petf"])
        cp("dve", PETB[:, :], PETF[:, :], ["petf"], ["petb"])
        for kv, n in enumerate(("cmp_k_w2", "cmp_v_w2")):
            dma("sp", W2T[:, kv, :, :], wb[n][l].rearrange("(c p) n -> p c n", p=128), [K("wb", n, l)], ["w2t"])

    FAKEYS = ["xt0", "xt1", "xt2", "gt0", "gt1", "gt2", "rx0", "rx1", "rx2", "ab0", "ab1", "tb0", "tb1"]

    def fa_alias(keys):
        for k_ in keys:
            sc.alias(k_, [o_ for o_ in FAKEYS if o_ != k_])

    def xkeys(t):
        return [("xres", t, p_) for p_ in ("h0", "h1", "q0", "q1", "q2", "q3")]

    def norm_phase(xsrc, gidx, to_dram=True, final_out=None):
        fa_alias(["xt0", "xt1", "xt2"])
        if final_out is not None:
            dma("sp", GBC[:, :], dr["ngb"][gidx], (), ["gbc"])
        else:
            dma("sp", GBC[:, :], dr["ngt"][gidx], (), ["gbc"])
        gbt = GBC[:, :].rearrange("p (c t) -> p c t", c=DC)

        def stL(t):
            dma("sp", XT[t % 3], xsrc[t * 128:(t + 1) * 128, :], xkeys(t), ["xt%d" % (t % 3)])

        SQB = [(SCR[:, :].bitcast(BF16), "scr"), (T1[:, :].bitcast(BF16), "t1")]

        def stA(t):
            xt, kx = XT[t % 3], "xt%d" % (t % 3)
            sq, ksq = SQB[t % 2]
            act(sq, xt, AF.Square, [kx], [ksq])
            sc.op("dve", lambda e, sq=sq, t=t: e.reduce_sum(out=SS[:, t:t + 1], in_=sq, axis=mybir.AxisListType.X), [ksq], [("ss", t)])

        def stB1(t):
            act(SS[:, 16 + t:17 + t], SS[:, t:t + 1], AF.Sqrt, [("ss", t)], [("rs", t)], bias=1e-6, scale=1.0 / D)
            sc.op("dve", lambda e, t=t: e.reciprocal(out=SS[:, 16 + t:17 + t], in_=SS[:, 16 + t:17 + t]), [("rs", t)], [("rs", t)])

        def stB(t):
            xt, kx = XT[t % 3], "xt%d" % (t % 3)
            hn, kh = HN[t % 2], "hn%d" % (t % 2)
            if final_out is not None:
                stt(xt, xt, SS[:, 16 + t:17 + t], GBC[:, :], ALU.mult, ALU.mult, [kx, ("rs", t), "gbc"], [kx])
                dma("pool", final_out[t * 128:(t + 1) * 128, :], xt, [kx], ["outd"])
                return
            act(hn[:, :], xt, AF.Copy, [kx, ("rs", t)], [kh], scale=SS[:, 16 + t:17 + t])
            pp = PS_P[t % 2]
            kp = "ps_p%d" % (t % 2)
            ppb = pp[:, :].bitcast(BF16)
            for c in range(DC):
                sc.op("pe", lambda e, c=c, ppb=ppb, hn=hn: e.transpose(ppb[:, c * 128:(c + 1) * 128], hn[:, c * 128:(c + 1) * 128], IDENT[:, :]),
                      [kh, "ident"], [kp])
            tg, t4 = divmod(t, 4)
            hs = HS[tg % 2]
            khs = "hs%d" % (tg % 2)
            tt("dve", hs[:, :, t4 * 128:(t4 + 1) * 128], ppb.rearrange("p (c t) -> p c t", c=DC), gbt, ALU.mult, [kp, "gbc"], [khs])
            if t4 == 3:
                dma("pool", hTd[:, :, tg * 512:(tg + 1) * 512], hs, [khs], ["hTd"])

        stL(0)
        stL(1)
        stA(0)
        for t in range(NT):
            if t + 2 < NT:
                stL(t + 2)
            stB1(t)
            if t + 1 < NT:
                stA(t + 1)
            stB(t)

    def load_hs(tg, i):
        dma("sp", HS[i], hTd[:, :, tg * 512:(tg + 1) * 512], ["hTd"], ["hs%d" % i])

    hs_ctr = [0]

    def next_hs(tg):
        i = hs_ctr[0] % 2
        hs_ctr[0] += 1
        load_hs(tg, i)
        return HS[i], "hs%d" % i

    pp_ctr = [0]
    pf_ctr = [0]

    def proj_fm(hs, khs, wv, c0, m, kw, evac):
        banks = [(PS_P[0], "ps_p0"), (PS_P[1], "ps_p1"), (PS_S[0], "ps_s0"), (PS_S[1], "ps_s1"), (PS_S[2], "ps_s2")]
        pp, kp = banks[pf_ctr[0] % 5]
        pf_ctr[0] += 1
        for dc in range(DC):
            mm(pp[0:m, :], wv[:, dc, c0:c0 + m], hs[:, dc, :], dc == 0, dc == DC - 1, [khs, kw], [kp])
        evac(pp, kp)

    def bias_ap(name, m, p0=0):
        return BCOL[p0:p0 + m, BIDX[name]:BIDX[name] + 1]

    st_ctr = [0]
    acc_ctr = [0]

    pending_fin = [None]

    def flush_fin():
        f = pending_fin[0]
        pending_fin[0] = None
        if f is not None:
            f()

    def defer_fin(f):
        flush_fin()
        pending_fin[0] = f

    def attn_group(steps, qt, kq, kt, kk, vfn):
        a = acc_ctr[0] % 2
        acc_ctr[0] += 1
        acc = PS_A[a]
        ka = "ps_a%d" % a
        if ZERO_INIT:
            mm(acc[:, :], ZEROB[:, :], CMASK[:, 0:512], True, False, ["zerob", "cmask"], [ka])
        pendq = []
        n = len(steps)
        for si, (kc0, q0, nq, mask) in enumerate(steps):
            s = st_ctr[0] % 3
            st_ctr[0] += 1
            pss = PS_S[s]
            ks = "ps_s%d" % s
            mm(pss[:, 0:nq], kt[:, kc0:kc0 + 128], qt[:, q0:q0 + nq], True, mask is None, [kk, kq], [ks])
            if mask is not None:
                map_, moff, mkey = mask
                w = map_.shape[1]
                mm(pss[:, moff:moff + w], IDENT[:, :], map_, False, True, ["ident", mkey], [ks])
            pt = PT[s]
            kpt = "pt%d" % s
            act(pt[:, 0:nq], pss[:, 0:nq], AF.Exp, [ks], [kpt])
            if si == min(1, n - 1):
                flush_fin()
            if len(pendq) >= 2:
                pendq.pop(0)()
            def pv(si=si, pt=pt, kpt=kpt, q0=q0, nq=nq, kc0=kc0):
                vl, kv = vfn(kc0)
                first = (si == 0) and not ZERO_INIT
                mm(acc[:, q0 % 512:q0 % 512 + nq], vl, pt[:, 0:nq], first, si == n - 1, [kpt, kv], [ka])
            pendq.append(pv)
        for f_ in pendq:
            f_()
        return acc, ka

    rb_ctr = [0]

    def recip_rows(src_ap, ksrc, pb, spb):
        i = rb_ctr[0] % 2
        rb_ctr[0] += 1
        rb = RB[i][pb:pb + 64, :]
        act(rb, src_ap, AF.Ln, [ksrc, "lnb"], ["rb%d" % i], bias=LNB[spb:spb + 64, 0:1])
        act(rb, rb, AF.Exp, ["rb%d" % i], ["rb%d" % i], scale=-1.0)
        return rb, "rb%d" % i

    def finalize_simple(acc, ka, pb, dst, kdst_w):
        rb, krb = recip_rows(acc[64 - pb:128 - pb, :], ka, pb, 64 - pb)
        tt("dve", dst, acc[pb:pb + 64, :], rb, ALU.mult, [ka, krb], [kdst_w])

    def ytile(h_global):
        return h_global // 2, (h_global % 2) * 64

    def fox_phase(l):
        wv = WM[:, 0:DC * 772].rearrange("p (c n) -> p c n", c=DC)
        dma("sp", wv, wb["w_in"][l, :, 0:772].rearrange("(c p) n -> p c n", p=128), [K("wb", "w_in", l)], ["wm"])
        for h in range(8):
            memset("pool", QK[h][64:128, :], 0.0, ["qk%d" % h])
            memset("pool", QK[h][64:68, :], 1.0, ["qk%d" % h])
        for tg in range(NTG):
            hs, khs = next_hs(tg)
            cs_ = slice(tg * 512, (tg + 1) * 512)
            for pr in range(2):
                def evq(pp, kp, pr=pr):
                    for hh in range(2):
                        h = 2 * pr + hh
                        ts("dve", QK[h][0:64, cs_], pp[64 * hh:64 * hh + 64, :], bias_ap("qa_p%d" % pr, 64, 64 * hh), 0.125, ALU.add, ALU.mult,
                           [kp, "bcol"], ["qk%d" % h])
                proj_fm(hs, khs, wv, C_QA + 128 * pr, 128, "wm", evq)
                def evk(pp, kp, pr=pr):
                    for hh in range(2):
                        h = 2 * pr + hh
                        act(QK[4 + h][0:64, cs_], pp[64 * hh:64 * hh + 64, :], AF.Identity, [kp, "bcol"], ["qk%d" % (4 + h)],
                            bias=bias_ap("ka_p%d" % pr, 64, 64 * hh))
                proj_fm(hs, khs, wv, C_KA + 128 * pr, 128, "wm", evk)
            def evf(pp, kp):
                ts("dve", FL, pp[0:4, :], bias_ap("fa", 4), -1.0, ALU.add, ALU.mult, [kp, "bcol"], ["t1"])
                act(FL, FL, AF.Exp, ["t1"], ["t1"])
                act(FL, FL, AF.Ln, ["t1"], ["t1"], bias=1.0)
            proj_fm(hs, khs, wv, C_FA, 4, "wm", evf)
            init = 0.0 if tg == 0 else CSP[:, 0:1]
            sc.op("dve", lambda e, init=init: e.tensor_tensor_scan(out=CS, data0=FL, data1=FL, initial=init,
                                                                    op0=ALU.add, op1=ALU.max), ["t1", "csp"], ["scr"])
            cp("dve", CSP[:, 0:1], SCR[0:4, 511:512], ["scr"], ["csp"])
            cp("dve", CK[:, 0, :], CS, ["scr"], ["ck"])
            tt("dve", CK[:, 1, :], CS, CK[:, 0, :], ALU.subtract, ["scr", "ck"], ["ck"])
            ts("dve", CQ[:, :, :], CK[:, :, :], -1.0, None, ALU.mult, None, ["ck"], ["cq"])
            for h in range(4):
                dma("pool", QK[h][64:66, cs_], CQ[h:h + 1, :, :], ["cq"], ["qk%d" % h])
                dma("pool", QK[4 + h][66:68, cs_], CK[h:h + 1, :, :], ["ck"], ["qk%d" % (4 + h)])
            for t4 in range(4):
                t = tg * 4 + t4
                i = pp_ctr[0] % 2
                pp_ctr[0] += 1
                pp = PS_P[i]
                kp = "ps_p%d" % i
                for dc in range(DC):
                    mm(pp[:, 0:256], hs[:, dc, t4 * 128:(t4 + 1) * 128], wv[:, dc, C_VA:C_VA + 256], dc == 0, False, [khs, "wm"], [kp])
                mm(pp[:, 0:256], ONESB[0:1, 0:128], VROW[0:1, 0:256], False, True, ["onesb", "vrow"], [kp])
                cp("act", VV[:, t, :, :].rearrange("p pr (j e) -> p pr j e", j=3)[:, :, 0:3:2, :],
                   pp[:, 0:256].rearrange("p (pr j e) -> p pr j e", pr=2, j=2), [kp], ["vv"])
        for h in range(4):
            ch, pb = ytile(h)
            for G in range(NTG):
                steps = []
                for j in range(4 * G + 4):
                    m = max(0, j - 4 * G)
                    q0 = (4 * G + m) * 128
                    nq = 512 - 128 * m
                    mask = (MASKAB[:, 0:128], 0, "maskab") if j >= 4 * G else None
                    steps.append((j * 128, q0, nq, mask))
                acc, ka = attn_group(steps, QK[h], "qk%d" % h, QK[4 + h], "qk%d" % (4 + h),
                                     lambda kc0, h=h: (VV[:, kc0 // 128, h // 2, (h % 2) * 64:(h % 2) * 64 + 128], "vv"))
                defer_fin(lambda acc=acc, ka=ka, pb=pb, ch=ch, G=G: finalize_simple(acc, ka, pb, YT[pb:pb + 64, ch, G * 512:(G + 1) * 512], "yt"))
        flush_fin()

    def dil_phase(l):
        for g, r in enumerate(DIL_R):
            Lc = S // r
            nbc = Lc // 128
            wv = WM[:, 0:DC * 384].rearrange("p (c n) -> p c n", c=DC)
            for j, c0 in enumerate((C_QB, C_KB, C_VB)):
                dma("sp", wv[:, :, j * 128:(j + 1) * 128],
                    wb["w_in"][l, :, c0 + 128 * g:c0 + 128 * g + 128].rearrange("(c p) n -> p c n", p=128),
                    [K("wb", "w_in", l)], ["wm"])
            for i_ in (0, 1, 4, 5):
                memset("pool", QK[i_][64:128, :], 0.0, ["qk%d" % i_])
            for p in range(2):
                dma("sp", QK[p][64:71, :], dr["qtab"][2 * g + p], (), ["qk%d" % p])
                dma("sp", QK[4 + p][64:71, :], dr["ktab"][g], (), ["qk%d" % (4 + p)])
            for tg in range(NTG):
                hs, khs = next_hs(tg)
                for j, (nm, dbase) in enumerate((("qb", 0), ("kb", 4), ("vb", 2))):
                    def ev(pp, kp, nm=nm, dbase=dbase):
                        for p in range(2):
                            dsti = dbase + p
                            dst = QK[dsti][0:64, :].rearrange("p (r a) -> p a r", r=r)[:, tg * (512 // r):(tg + 1) * (512 // r), :]
                            src_ = pp[64 * p:64 * p + 64, :].rearrange("p (a r) -> p a r", r=r)
                            if nm == "qb":
                                ts("dve", dst, src_, bias_ap("qb_g%d" % g, 64, 64 * p), 0.125, ALU.add, ALU.mult, [kp, "bcol"], ["qk%d" % dsti])
                            else:
                                act(dst, src_, AF.Identity, [kp, "bcol"], ["qk%d" % dsti], bias=bias_ap("%s_g%d" % (nm, g), 64, 64 * p))
                    proj_fm(hs, khs, wv, j * 128, 128, "wm", ev)
            for p in range(2):
                for ub in range(16):
                    i = pp_ctr[0] % 2
                    pp_ctr[0] += 1
                    ppb = PS_P[i][:, :].bitcast(BF16)
                    kp = "ps_p%d" % i
                    sc.op("pe", lambda e, ppb=ppb, p=p, ub=ub: e.transpose(ppb[:, 0:64], QK[2 + p][0:64, ub * 128:(ub + 1) * 128], IDENT[0:64, 0:64]),
                          ["qk%d" % (2 + p), "ident"], [kp])
                    cp("act", VV[:, ub, 0, p * 128:p * 128 + 64], ppb[:, 0:64], [kp], ["vv"])
            for p in range(2):
                h = 2 * g + p
                ch, pb = ytile(4 + h)
                for Gq in range(NTG):
                    steps = []
                    for kb in range(4 * Gq - 1, 4 * Gq + 4):
                        if kb < 0:
                            continue
                        ql = [q for q in (kb, kb + 1) if 4 * Gq <= q < 4 * Gq + 4 and q // nbc == kb // nbc]
                        if not ql:
                            continue
                        if ql == [kb, kb + 1]:
                            mask = (MASKAB[:, 0:256], 0, "maskab")
                        elif ql == [kb]:
                            mask = (MASKAB[:, 0:128], 0, "maskab")
                        else:
                            mask = (MASKAB[:, 128:256], 0, "maskab")
                        steps.append((kb * 128, ql[0] * 128, 128 * len(ql), mask))
                    acc, ka = attn_group(steps, QK[p], "qk%d" % p, QK[4 + p], "qk%d" % (4 + p),
                                         lambda kc0, p=p: (VV[:, kc0 // 128, 0, p * 64:p * 64 + 128], "vv"))
                    def dil_fin(acc=acc, ka=ka, Gq=Gq, p=p, g=g, r=r, pb=pb, ch=ch):
                        if r == 1:
                            r0, nr, a0, na = 0, 1, 512 * Gq, 512
                        elif r == 4:
                            r0, nr, a0, na = Gq, 1, 0, 512
                        else:
                            r0, nr, a0, na = 4 * Gq, 4, 0, 128
                        lp = 64 - pb
                        dsty = YT[pb:pb + 64, ch, :].rearrange("p (a r) -> p r a", r=r)[:, r0:r0 + nr, a0:a0 + na]
                        cp("act", dsty, acc[pb:pb + 64, :].rearrange("p (r a) -> p r a", r=nr), [ka], ["yt", "accsync"])
                        dstl = LS[lp:lp + 64, :].rearrange("p (a r) -> p r a", r=r)[:, r0:r0 + nr, a0:a0 + na]
                        srcl = acc[lp:lp + 64, :].rearrange("p (r a) -> p r a", r=nr)
                        if g == 0:
                            cp("dve", dstl, srcl, [ka, "accsync"], ["ls%d" % p])
                        else:
                            tt("dve", dstl, dstl, srcl, ALU.add, [ka, "accsync", "ls%d" % p], ["ls%d" % p])
                    defer_fin(dil_fin)
        flush_fin()
        for p in range(2):
            pb = 64 * p
            lp = 64 - pb
            for tg in range(NTG):
                cs_ = slice(tg * 512, (tg + 1) * 512)
                rb, krb = recip_rows(LS[lp:lp + 64, cs_], "ls%d" % p, pb, lp)
                for g in range(3):
                    ch, _pb = ytile(4 + 2 * g + p)
                    yv = YT[pb:pb + 64, ch, cs_]
                    tt("dve", yv, yv, rb, ALU.mult, ["yt", krb], ["yt"])

    def nsa_phase(l):
        NWC = 192 + 6 * 64 + 9
        OQ, OKSL, OKWN, OKC, OVC, OVSL, OVWN, OGC = 0, 192, 256, 320, 384, 448, 512, 576
        for g in range(2):
            wv = WM[:, 0:DC * NWC].rearrange("p (c n) -> p c n", c=DC)
            segs = [(OQ, C_QC + 192 * g, 192), (OKSL, C_KSL + 64 * g, 64), (OKWN, C_KWN + 64 * g, 64), (OKC, C_KC + 64 * g, 64),
                    (OVC, C_VC + 64 * g, 64), (OVSL, C_VSL + 64 * g, 64), (OVWN, C_VWN + 64 * g, 64), (OGC, C_GC + 9 * g, 9)]
            for (o_, c0, n_) in segs:
                dma("sp", wv[:, :, o_:o_ + n_], wb["w_in"][l, :, c0:c0 + n_].rearrange("(c p) n -> p c n", p=128),
                    [K("wb", "w_in", l)], ["wm"])
            for i in (0, 1, 2, 4, 5):
                memset("pool", QK[i][64:128, :], 0.0, ["qk%d" % i])
            for i in range(3):
                dma("sp", QK[i][64:71, :], dr["qtab"][6 + 3 * g + i], (), ["qk%d" % i])
            dma("sp", QK[4][64:71, :], dr["ktab"][3], (), ["qk4"])
            dma("sp", QK[5][64:71, :], dr["ktab"][3], (), ["qk5"])
            dma("sp", QK[4][96:128, :], dr["ksel"][:, :], (), ["qk4"])
            dma("sp", KCT[64:71, :], dr["ktab"][4, :, 0:128], (), ["kct"])
            for tg in range(NTG):
                hs, khs = next_hs(tg)
                cs_ = slice(tg * 512, (tg + 1) * 512)
                plan = [("n0", [(0, 0, True), (1, 1, True)]), ("n1", [(0, 2, True), (1, 4, False)]),
                        ("n2", [(0, 5, False), (1, 3, False)]), ("n3", [(0, 6, False)])]
                for gi, (bn, parts) in enumerate(plan):
                    def evn(pp, kp, bn=bn, parts=parts):
                        for (hh, dsti, isq) in parts:
                            if isq:
                                ts("dve", QK[dsti][0:64, cs_], pp[64 * hh:64 * hh + 64, :], bias_ap("%s_g%d" % (bn, g), 64, 64 * hh), 0.125,
                                   ALU.add, ALU.mult, [kp, "bcol"], ["qk%d" % dsti])
                            elif any(p_[2] for p_ in parts):
                                ts("dve", QK[dsti][0:64, cs_], pp[64 * hh:64 * hh + 64, :], bias_ap("%s_g%d" % (bn, g), 64, 64 * hh), None,
                                   ALU.add, None, [kp, "bcol"], ["qk%d" % dsti])
                            else:
                                act(QK[dsti][0:64, cs_], pp[64 * hh:64 * hh + 64, :], AF.Identity, [kp, "bcol"], ["qk%d" % dsti],
                                    bias=bias_ap("%s_g%d" % (bn, g), 64, 64 * hh))
                    proj_fm(hs, khs, wv, 128 * gi, 128, "wm", evn)
                def evg(pp, kp):
                    act(GSB[64:73, cs_], pp[64:73, :], AF.Sigmoid, [kp, "bcol"], ["gsb"], bias=bias_ap("gc%d" % g, 9, 64))
                proj_fm(hs, khs, wv, OGC - 64, 73, "wm", evg)
                for t4 in range(4):
                    t = tg * 4 + t4
                    i = pp_ctr[0] % 2
                    pp_ctr[0] += 1
                    pp = PS_P[i]
                    kp = "ps_p%d" % i
                    for dc in range(DC):
                        mm(pp[:, 0:128], hs[:, dc, t4 * 128:(t4 + 1) * 128], wv[:, dc, OVSL:OVSL + 128], dc == 0, False, [khs, "wm"], [kp])
                    mm(pp[:, 0:128], ONESB[0:1, 0:128], VROW[0:1, 256 + 128 * g:384 + 128 * g], False, True, ["onesb", "vrow"], [kp])
                    vv3 = VV[:, t, :, :].rearrange("p pr (j e) -> p pr j e", j=3)
                    cp("act", vv3[:, :, 0, :], pp[:, 0:128].rearrange("p (h e) -> p h e", h=2), [kp], ["vv"])
                    cp("pool", vv3[:, :, 2, :], vv3[:, :, 0, :], ["vv"], ["vv"])
            for kv, (src, wn) in enumerate(((3, "cmp_k_w1"), (6, "cmp_v_w1"))):
                srcT = QK[src]
                ksrc = "qk%d" % src
                for sl in range(4):
                    wi = sl % 2
                    w1v = W01[wi][0:64, 0:8 * 256].rearrange("p (l n) -> p l n", l=8)
                    dma("sp", w1v, wb[wn][l, sl * 512:(sl + 1) * 512, :].rearrange("(l p) n -> p l n", p=64),
                        [K("wb", wn, l)], ["w01_%d" % wi])
                    for ll in range(8):
                        lidx = sl * 8 + ll
                        for mb in range(2):
                            mm(PS_P[mb][:, 0:127], w1v[:, ll, mb * 128:(mb + 1) * 128], srcT[0:64, lidx:lidx + 16 * 126 + 1:16],
                               lidx == 0, lidx == 31, ["w01_%d" % wi, ksrc], ["ps_p%d" % mb])
                            mm(PS_S[mb][:, 0:1], w1v[:, ll, mb * 128:(mb + 1) * 128], PETB[0:64, lidx:lidx + 1],
                               lidx == 0, lidx == 31, ["w01_%d" % wi, "petb"], ["ps_s%d" % mb])
                for mb in range(2):
                    cp("dve", HB[:, mb:mb + 1], PS_S[mb][:, 0:1], ["ps_s%d" % mb], ["hb"])
                    act(HG[kv][:, mb, 0:127], PS_P[mb][:, 0:127], AF.Gelu_apprx_tanh, ["ps_p%d" % mb, "hb"], ["hg%d" % kv],
                        bias=HB[:, mb:mb + 1])
            for mb in range(2):
                mm(PS_B[0:64, 0:127], W2T[:, 0, mb, :], HG[0][:, mb, 0:127], mb == 0, mb == 1, ["w2t", "hg0"], ["ps_b"])
            cp("act", KCT[0:64, 0:127], PS_B[0:64, 0:127], ["ps_b"], ["kct"])
            for mb in range(2):
                mm(PS_B[0:127, 128:192], HG[1][:, mb, 0:127], W2T[:, 1, mb, :], mb == 0, mb == 1, ["w2t", "hg1"], ["ps_b"])
            cp("act", VC[0:127, 0:64], PS_B[0:127, 128:192], ["ps_b"], ["vc"])
            cp("pool", VC[0:127, 128:192], VC[0:127, 0:64], ["vc"], ["vc"])

            def nsa_final(acc, ka, i, b, Gq, has_tiny):
                cs2 = slice(Gq * 512, (Gq + 1) * 512)
                ch, pb = ytile(10 + 3 * g + i)
                yv = YT[pb:pb + 64, ch, cs2]
                rb, krb = recip_rows(acc[64 - pb:128 - pb, :], ka, pb, 64 - pb)
                mm(PS_B[:, :], SELG[64:128, 3 * i + b, :], GSB[64:128, cs2], True, True, ["selg", "gsb"], ["ps_b"])
                tt("dve", T1[pb:pb + 64, :], PS_B[pb:pb + 64, :], rb, ALU.mult, ["ps_b", krb], ["t1"])
                if b == 0:
                    tt("dve", yv, acc[pb:pb + 64, :], T1[pb:pb + 64, :], ALU.mult, [ka, "t1"], ["yt"])
                else:
                    tt("dve", TMPB[b][pb:pb + 64, :], acc[pb:pb + 64, :], T1[pb:pb + 64, :], ALU.mult, [ka, "t1"], ["tmpb%d" % b])
                if b == 2:
                    tt("pool", SCR[pb:pb + 64, :], yv, TMPB[1][pb:pb + 64, :], ALU.add, ["yt", "tmpb1"], ["scr"])
                    tt("pool", yv, SCR[pb:pb + 64, :], TMPB[2][pb:pb + 64, :], ALU.add, ["scr", "tmpb2"], ["yt"])

            def vsel(slot, i):
                par = (10 + 3 * g + i) % 2
                return lambda kc0: (VV[:, kc0 // 128, slot, par * 64:par * 64 + 128], "vv")

            for i in range(3):
                for Gq in range(NTG):
                    cs2 = slice(Gq * 512, (Gq + 1) * 512)
                    steps = [(0, Gq * 512, 512, (CMASK[:, cs2], 0, "cmask"))]
                    s_before = st_ctr[0] % 3
                    par = (10 + 3 * g + i) % 2
                    acc, ka = attn_group(steps, QK[i], "qk%d" % i, KCT, "kct", lambda kc0, par=par: (VC[:, par * 64:par * 64 + 128], "vc"))
                    pt = PT[s_before]
                    kpt = "pt%d" % s_before
                    for qb in range(4):
                        mm(PS_P[0][:, qb * 33:qb * 33 + 33], pt[:, qb * 128:(qb + 1) * 128], OV33[:, :], True, True,
                           [kpt, "ov33"], ["ps_p0"])
                    imv = PS_P[0][:, 0:132].rearrange("p (q e) -> p q e", q=4)
                    ts("dve", RI[:, :], imv[:, :, 32], 1e-30, None, ALU.max, None, ["ps_p0"], ["ri"])
                    sc.op("dve", lambda e: e.reciprocal(out=RI[:, :], in_=RI[:, :]), ["ri"], ["ri"])
                    for qb in range(4):
                        t = Gq * 4 + qb
                        dsti = IMPS[:, t * 32:(t + 1) * 32]
                        if i == 0:
                            ts("dve", dsti, imv[:, qb, 0:32], RI[:, qb:qb + 1], None, ALU.mult, None, ["ps_p0", "ri"], ["imps"])
                        else:
                            stt(dsti, imv[:, qb, 0:32], RI[:, qb:qb + 1], dsti, ALU.mult, ALU.add, ["ps_p0", "ri", "imps"], ["imps"])
                    defer_fin(lambda acc=acc, ka=ka, i=i, Gq=Gq: nsa_final(acc, ka, i, 0, Gq, True))
            flush_fin()
            tt("dve", SCR[:, :], IMPS[:, :], M1[:, :], ALU.mult, ["imps", "m1"], ["scr"])
            tt("dve", SCR[:, :], SCR[:, :], M2[:, :], ALU.add, ["scr", "m2"], ["scr"])
            for t in range(NT):
                sc.op("dve", lambda e, t=t: e.max(out=MX[:, t, :], in_=SCR[:, t * 32:(t + 1) * 32]), ["scr"], ["mx"])
            for t in range(NT):
                ts("dve", SCR[:, t * 32:(t + 1) * 32], SCR[:, t * 32:(t + 1) * 32], MX[:, t, 7:8], None, ALU.is_ge, None, ["scr", "mx"], ["scr"])
            ts("dve", NSEL[:, :], SCR[:, :], -NEG, NEG, ALU.mult, ALU.add, ["scr"], ["nsel"])
            for tg in range(NTG):
                ppb = PS_P[tg % 2][:, :].bitcast(BF16)
                kp = "ps_p%d" % (tg % 2)
                for t4 in range(4):
                    t = tg * 4 + t4
                    sc.op("pe", lambda e, ppb=ppb, t=t, t4=t4: e.transpose(ppb[0:32, t4 * 128:(t4 + 1) * 128], NSEL[:, t * 32:(t + 1) * 32], IDENT[:, :]),
                          ["nsel", "ident"], [kp])
                cs_ = slice(tg * 512, (tg + 1) * 512)
                cp("act", QK[0][96:128, cs_], ppb[0:32, 0:512], [kp], ["qk0"])
                cp("pool", QK[1][96:128, cs_], QK[0][96:128, cs_], ["qk0"], ["qk1"])
                cp("pool", QK[2][96:128, cs_], QK[0][96:128, cs_], ["qk0"], ["qk2"])
            for i in range(3):
                for Gq in range(NTG):
                    steps = []
                    for j in range(4 * Gq + 4):
                        m = max(0, j - 4 * Gq)
                        q0 = (4 * Gq + m) * 128
                        nq = 512 - 128 * m
                        mask = (MASKAB[:, 0:128], 0, "maskab") if j >= 4 * Gq else None
                        steps.append((j * 128, q0, nq, mask))
                    acc, ka = attn_group(steps, QK[i], "qk%d" % i, QK[4], "qk4", vsel(0, i))
                    defer_fin(lambda acc=acc, ka=ka, i=i, Gq=Gq: nsa_final(acc, ka, i, 1, Gq, False))
                    steps = []
                    for j in range(max(0, 4 * Gq - 4), 4 * Gq + 4):
                        if j < 4 * Gq:
                            qhi = min(4 * Gq + 3, j + 4)
                            q0 = 4 * Gq * 128
                            nq = (qhi - 4 * Gq + 1) * 128
                            mask = (MASKAB[:, 256:384], nq - 128, "maskab") if qhi == j + 4 else None
                        else:
                            q0 = j * 128
                            nq = (4 * Gq + 4 - j) * 128
                            mask = (MASKAB[:, 0:128], 0, "maskab")
                        steps.append((j * 128, q0, nq, mask))
                    acc, ka = attn_group(steps, QK[i], "qk%d" % i, QK[5], "qk5", vsel(1, i))
                    defer_fin(lambda acc=acc, ka=ka, i=i, Gq=Gq: nsa_final(acc, ka, i, 2, Gq, False))
            flush_fin()


    RXM = [FA[:, 1536 + i * 512:2048 + i * 512] for i in range(3)]
    RXF = [FA[:, 2052 + i * 256:2308 + i * 256] for i in range(3)]
    rx_ctr = [0]
    YTF = YT.rearrange("p c t -> p (c t)")

    def merge_phase(l, xsrc):
        wp = WM[:, 0:8192].rearrange("p (c n) -> p c n", c=8)
        dma("sp", wp[:, 0:2, :], wb["w_pa"][l].rearrange("(c p) n -> p c n", p=128), [K("wb", "w_pa", l)], ["wm"])
        dma("sp", wp[:, 2:5, :], wb["w_pb"][l].rearrange("(c p) n -> p c n", p=128), [K("wb", "w_pb", l)], ["wm"])
        dma("sp", wp[:, 5:8, :], wb["w_pc"][l].rearrange("(c p) n -> p c n", p=128), [K("wb", "w_pc", l)], ["wm"])
        qkeys = ["qk%d" % i for i in range(8)]
        WO = QKALL[:, 0:8192].rearrange("p (c n) -> p c n", c=8)
        dma("sp", WO, wb["w_o"][l].rearrange("(c p) n -> p c n", p=128), [K("wb", "w_o", l)], qkeys)
        MGS = [QKALL[:, 8192 + i * 4096:12288 + i * 4096].rearrange("p (c t) -> p c t", c=8) for i in range(2)]
        sc.alias("mg0", ["qk4", "qk5"])
        sc.alias("mg1", ["qk6", "qk7"])
        for k_ in ("gsl0", "gsl1", "gsl2"):
            sc.alias(k_, ["w01_0", "w01_1"])
        fa_alias(["gt0", "gt1", "gt2", "rx0", "rx1", "rx2"])
        GT = [FA[:, b * 512:(b + 1) * 512] for b in range(3)]
        brch = [(0, 2), (2, 5), (5, 8)]
        brps = [(PS_A[0], "ps_a0"), (PS_A[1], "ps_a1"), (PS_B, "ps_b")]
        items = [(t4, half) for t4 in range(4) for half in range(2)]
        rxs = {}

        def ld(tg_, n_):
            t4_, half_ = items[n_]
            t_ = tg_ * 4 + t4_
            ri = rx_ctr[0] % 3
            rx_ctr[0] += 1
            rxs[(tg_, n_)] = (RXM[ri], "rx%d" % ri)
            dma("sp", RXM[ri], xsrc[t_ * 128:(t_ + 1) * 128, half_ * 512:(half_ + 1) * 512], xkeys(t_), ["rx%d" % ri])

        def wo_item(tg_, n_):
            MG, kmg = MGS[tg_ % 2], "mg%d" % (tg_ % 2)
            t4, half = items[n_]
            t = tg_ * 4 + t4
            if n_ == 0:
                ld(tg_, 0)
                ld(tg_, 1)
            pp = PS_P[half]
            kp = "ps_p%d" % half
            for c in range(8):
                mm(pp[:, :], MG[:, c, t4 * 128:(t4 + 1) * 128], WO[:, c, half * 512:(half + 1) * 512], c == 0, c == 7,
                   [kmg] + qkeys[0:4], [kp])
            if n_ + 2 < len(items):
                ld(tg_, n_ + 2)
            rx, krx = rxs[(tg_, n_)]
            tt("dve", rx, rx, pp[:, :], ALU.add, [krx, kp], [krx])
            dma("pool", xres[t * 128:(t + 1) * 128, half * 512:(half + 1) * 512], rx, [krx], [("xres", t, "h%d" % half)])

        for tg in range(NTG):
            hs, khs = next_hs(tg)
            cs_ = slice(tg * 512, (tg + 1) * 512)
            MG, kmg = MGS[tg % 2], "mg%d" % (tg % 2)
            for c in range(8):
                wi = gs_ctr[0] % 3
                gs_ctr[0] += 1
                gflat = W01ALL[:, wi * 3072:(wi + 1) * 3072]
                gsl = gflat.rearrange("p (b c n) -> p b c n", b=3, c=DC)
                dma("sp", gflat, wg[l, c], [K("wg", l)], ["gsl%d" % wi])
                for b in range(3):
                    for dc in range(DC):
                        mm(PS_S[b][:, :], gsl[:, b, dc, :], hs[:, dc, :], dc == 0, dc == DC - 1, ["gsl%d" % wi, khs], ["ps_s%d" % b])
                    k0, k1 = brch[b]
                    bp, kbp = brps[b]
                    for kc in range(k0, k1):
                        mm(bp[:, :], wp[:, kc, c * 128:(c + 1) * 128], YT[:, kc, cs_], kc == k0, kc == k1 - 1, ["wm", "yt"], [kbp])
                if tg >= 1:
                    wo_item(tg - 1, c)
                for b in range(3):
                    bp, kbp = brps[b]
                    act(GT[b], PS_S[b][:, :], AF.Sigmoid, ["ps_s%d" % b, "bcol"], ["gt%d" % b], bias=bias_ap("gt%d" % (b * 8 + c), 128))
                    tt("dve", GT[b], GT[b], bp[:, :], ALU.mult, ["gt%d" % b, kbp], ["gt%d" % b])
                tt("pool", GT[0], GT[0], GT[1], ALU.add, ["gt0", "gt1"], ["gt0"])
                tt("pool", MG[:, c, :], GT[0], GT[2], ALU.add, ["gt0", "gt2"], [kmg])
        for n_ in range(8):
            wo_item(NTG - 1, n_)

    def ffn_phase(l):
        memset("pool", AH[:, :, :], 0.0, ["ah"])
        sc.alias("yt2", ["yt"])
        for k_ in ("w01_0", "w01_1"):
            sc.alias(k_, ["gsl0", "gsl1", "gsl2"])
        fa_alias(["ab0", "ab1", "tb0", "tb1", "rx0", "rx1", "rx2"])
        for k_ in ("qk4", "qk5"):
            sc.alias(k_, ["mg0"])
        for k_ in ("qk6", "qk7"):
            sc.alias(k_, ["mg1"])
        qkeys = ["qk%d" % i for i in range(8)]
        ACTT = QKALL[:, 0:FC * 512].rearrange("p (c t) -> p c t", c=FC)
        AB = [FA[:, i * 514:(i + 1) * 514] for i in range(2)]
        TB = [FA[:, 1028 + i * 512:1028 + (i + 1) * 512] for i in range(2)]
        B6 = [(PS_S[0], "ps_s0"), (PS_S[1], "ps_s1"), (PS_S[2], "ps_s2"), (PS_A[0], "ps_a0"), (PS_A[1], "ps_a1"), (PS_B, "ps_b")]
        DW = [WM[:, 0:5632], YTF[:, 0:5632], YTF[:, 5632:11264]]
        dwk = ["wm", "yt", "yt2"]
        dw_ctr = 0
        cwv = CW[:, :].rearrange("p (c k) -> p c k", k=4)
        it = 0
        def load_dw(qd_):
            nonlocal dw_ctr
            di_ = dw_ctr % 3
            dw_ctr += 1
            dma("sp", DW[di_], wd[l, qd_], [K("wd", l)], [dwk[di_]])
            return di_

        for tg in range(NTG):
            hs, khs = next_hs(tg)
            dslots = [load_dw(0), load_dw(1)]
            slabs = {}

            def stP(fc, it_):
                fp, sub = divmod(fc, 2)
                if sub == 0:
                    wi = fp % 2
                    dma("sp", W01[wi][:, 0:4096], wu[l, fp], [K("wu", l)], ["w01_%d" % wi])
                    slabs[fp] = (W01[wi][:, 0:4096].rearrange("p (c b n) -> p c b n", c=DC, b=2), "w01_%d" % wi)
                usl, kw = slabs[fp]
                pa, kpa = B6[(2 * it_) % 6]
                pb_, kpb = B6[(2 * it_ + 1) % 6]
                for dc in range(DC):
                    mm(pa[:, :], usl[:, dc, 0, sub * 128:(sub + 1) * 128], hs[:, dc, :], dc == 0, dc == DC - 1, [kw, khs], [kpa])
                for dc in range(DC):
                    mm(pb_[:, :], usl[:, dc, 1, sub * 128:(sub + 1) * 128], hs[:, dc, :], dc == 0, dc == DC - 1, [kw, khs], [kpb])

            def stE1(fc, it_):
                pa, kpa = B6[(2 * it_) % 6]
                A, kA = AB[it_ % 2], "ab%d" % (it_ % 2)
                T, kT = TB[it_ % 2], "tb%d" % (it_ % 2)
                cp("pool", A[:, 0:2], AH[:, fc, :], ["ah"], [kA])
                cp("act", A[:, 2:514], pa[:, :], [kpa], [kA])
                cp("pool", AH[:, fc, :], A[:, 512:514], [kA], ["ah"])
                ts("dve", T, A[:, 0:512], cwv[:, fc, 0:1], cwv[:, fc, 3:4], ALU.mult, ALU.add, [kA, "cw"], [kT])
                stt(T, A[:, 1:513], cwv[:, fc, 1:2], T, ALU.mult, ALU.add, [kA, "cw", kT], [kT], relaxed=True)
                stt(T, A[:, 2:514], cwv[:, fc, 2:3], T, ALU.mult, ALU.add, [kA, "cw", kT], [kT], relaxed=True)

            def stE2(fc, it_):
                pb_, kpb = B6[(2 * it_ + 1) % 6]
                T, kT = TB[it_ % 2], "tb%d" % (it_ % 2)
                act(T, T, AF.Gelu_apprx_tanh, [kT], [kT])
                tt("dve", ACTT[:, fc, :], T, pb_[:, :], ALU.mult, [kT, kpb], qkeys[0:6])

            for fc in range(FC):
                stP(fc, it + fc)
                stE1(fc, it + fc)
                if fc >= 1:
                    stE2(fc - 1, it + fc - 1)
            stE2(FC - 1, it + FC - 1)
            it += FC
            accs = [(PS_P[0], "ps_p0"), (PS_P[1], "ps_p1"), (PS_P[0], "ps_p0"), (PS_P[1], "ps_p1")]
            ditems = [(qd, t4) for qd in range(4) for t4 in range(4)]
            rxs = {}

            def ldf(n_):
                qd_, t4_ = ditems[n_]
                t_ = tg * 4 + t4_
                ri = rx_ctr[0] % 3
                rx_ctr[0] += 1
                rxs[n_] = (RXF[ri], "rx%d" % ri)
                dma("sp", RXF[ri], xres[t_ * 128:(t_ + 1) * 128, qd_ * 256:(qd_ + 1) * 256], xkeys(t_), ["rx%d" % ri])

            ldf(0)
            ldf(1)
            for n_, (qd, t4) in enumerate(ditems):
                if t4 == 0 and qd + 2 < 4:
                    dslots.append(load_dw(qd + 2))
                di = dslots[qd]
                dsl = DW[di].rearrange("p (c n) -> p c n", c=FC)
                t = tg * 4 + t4
                pp, kp = accs[t4]
                for fc in range(FC):
                    mm(pp[:, 0:256], ACTT[:, fc, t4 * 128:(t4 + 1) * 128], dsl[:, fc, :], fc == 0, fc == FC - 1, qkeys[0:6] + [dwk[di]], [kp])
                if n_ + 2 < len(ditems):
                    ldf(n_ + 2)
                rx, krx = rxs[n_]
                tt("dve", rx, rx, pp[:, 0:256], ALU.add, [krx, kp], [krx])
                dma("pool", xres[t * 128:(t + 1) * 128, qd * 256:(qd + 1) * 256], rx, [krx], [("xres", t, "q%d" % qd)])

    def driver():
        for s_ in range(nseq):
            for l in range(depth):
                load_layer_small(l)
                sc.alias("yt", ["yt2"])
                for k_ in ("w01_0", "w01_1"):
                    sc.alias(k_, ["gsl0", "gsl1", "gsl2"])
                for k_ in ("qk4", "qk5"):
                    sc.alias(k_, ["mg0"])
                for k_ in ("qk6", "qk7"):
                    sc.alias(k_, ["mg1"])
                xsrc = x_in[s_] if l == 0 else xres
                norm_phase(xsrc, 2 * l)
                if stop_after == "norm1":
                    return ["hTd"]
                if "fox" in enable:
                    fox_phase(l)
                else:
                    memset("pool", YTF[:, 0:2 * S], 0.0, ["yt"])
                if s_ == 0 and l == 0:
                    relayout(0)
                if "dil" in enable:
                    dil_phase(l)
                else:
                    memset("pool", YTF[:, 2 * S:5 * S], 0.0, ["yt"])
                if "nsa" in enable:
                    nsa_phase(l)
                else:
                    memset("pool", YTF[:, 5 * S:8 * S], 0.0, ["yt"])
                if s_ == 0 and l == 0:
                    for l2 in range(1, depth):
                        relayout(l2)
                if stop_after == "mix":
                    dma("sp", ytd, YT, ["yt"], ["ytd"])
                    return ["ytd"]
                merge_phase(l, xsrc)
                if stop_after == "merge":
                    return [k_ for t_ in range(NT) for k_ in xkeys(t_)]
                norm_phase(xres, 2 * l + 1)
                if stop_after == "norm2":
                    return ["hTd"]
                ffn_phase(l)
                if stop_after == "ffn":
                    return [k_ for t_ in range(NT) for k_ in xkeys(t_)]
            norm_phase(xres, 2 * depth, final_out=out[s_])
        return ["outd"]

    fkeys = driver()
    sc.final_wait("sp", fkeys)
    global LAST_SCHED
    LAST_SCHED = sc
    sc.emit()
    es.close()
    return nc


def _host_inputs(inp, depth):
    h = {}
    g = [inp["norm1_g"], inp["norm2_g"]]
    ngb = np.zeros((2 * depth + 1, 128, D), np.float32)
    for l in range(depth):
        ngb[2 * l] = np.broadcast_to(g[0][l], (128, D))
        ngb[2 * l + 1] = np.broadcast_to(g[1][l], (128, D))
    ngb[2 * depth] = np.broadcast_to(inp["final_g"], (128, D))
    h["ngb"] = ngb
    gl = [ngb[i, 0] for i in range(2 * depth + 1)]
    h["ngt"] = np.stack([np.repeat(g_.reshape(DC, 128).T[:, :, None], 128, axis=2).reshape(128, D) for g_ in gl]).astype(np.float32)
    bcol = np.zeros((depth, 128, NB), np.float32)
    for l in range(depth):
        for i, (n, segs) in enumerate(BIAS_BLOCKS):
            for (p0, c0, m) in segs:
                bcol[l, p0:p0 + m, i] = inp["b_in"][l, c0:c0 + m]
    h["bcol"] = bcol
    h["vrow"] = np.ascontiguousarray(inp["b_in"][:depth][:, None, VROW_COLS]).astype(np.float32)
    cw = np.zeros((depth, 128, FC, 4), np.float32)
    for l in range(depth):
        for k in range(3):
            cw[l, :, :, k] = inp["conv_w"][l, k].reshape(FC, 128).T
        cw[l, :, :, 3] = inp["conv_b"][l].reshape(FC, 128).T
    h["cw"] = cw.reshape(depth, 128, FC * 4)
    h["pet"] = np.ascontiguousarray(np.transpose(inp["cmp_pe"][:depth], (0, 2, 1))).astype(np.float32)
    return h


_CACHE = {}


def kernel(**inputs):
    ncores = 8
    depth = inputs["w_in"].shape[0]
    B = inputs["x"].shape[0]
    nseq = B // ncores
    key = (nseq, depth)
    if key not in _CACHE:
        _CACHE[key] = build_program(nseq, depth)
    nc = _CACHE[key]
    consts = _make_consts()
    hi = _host_inputs(inputs, depth)
    common = {}
    for n, _ in WEIGHTS:
        common[n] = np.ascontiguousarray(inputs[n], dtype=np.float32)
    common.update(hi)
    common.update(consts)
    x = np.ascontiguousarray(inputs["x"], dtype=np.float32)
    in_maps = []
    for c in range(ncores):
        m = dict(common)
        m["x"] = x[c * nseq:(c + 1) * nseq]
        in_maps.append(m)
    res = run_bass_kernel_spmd(nc, in_maps, core_ids=list(range(ncores)))
    return np.concatenate([r["out"] for r in res.results], axis=0)
```

```python
import math
from contextlib import ExitStack
import numpy as np
import ml_dtypes
import concourse.bass as bass
import concourse.mybir as mybir
from concourse.bass_utils import run_bass_kernel_spmd

F32 = mybir.dt.float32
BF16 = mybir.dt.bfloat16
AF = mybir.ActivationFunctionType
ALU = mybir.AluOpType
NPBF = ml_dtypes.bfloat16

S = 2048
D = 1024
NT = 16
DC = 8
NTG = 4
DIN = 6166
DFF = 2816
FC = 22
NEG = -30000.0
C_QA, C_KA, C_VA, C_FA = 0, 256, 512, 768
C_QB, C_KB, C_VB = 772, 1156, 1540
C_QC, C_KC, C_VC, C_KSL, C_VSL, C_KWN, C_VWN, C_GC, C_GATES = 1924, 2308, 2436, 2564, 2692, 2820, 2948, 3076, 3094
DIL_R = (1, 4, 16)
ZERO_INIT = False


class Sched:
    def __init__(self, nc, es, n_dma=32):
        self.nc = nc
        self.engs = {"pe": "tensor", "act": "scalar", "dve": "vector", "pool": "gpsimd", "sp": "sync"}
        self.ops = {e: [] for e in self.engs}
        self.sem = {e: es.enter_context(nc.semaphore("s_" + e)) for e in self.engs}
        self.dsem = [es.enter_context(nc.semaphore("d%d" % i)) for i in range(n_dma)]
        self.dcount = [0] * n_dma
        self.dnext = 0
        self.lastw = {}
        self.readers = {}
        self.known = {e: {} for e in self.engs}
        self.targets = {e: set() for e in self.engs}

    def _deps(self, reads, writes):
        toks = []
        for k in reads:
            toks.extend(self.lastw.get(k, {}).values())
        for k in writes:
            toks.extend(self.lastw.get(k, {}).values())
            toks.extend(self.readers.get(k, {}).values())
        return toks

    def _waits(self, eng, toks, same_ok):
        w = {}
        for (s, i) in toks:
            if same_ok and s == eng:
                continue
            if self.known[eng].get(s, -1) >= i:
                continue
            if w.get(s, -1) < i:
                w[s] = i
        for s, i in w.items():
            self.known[eng][s] = i
            if not isinstance(s, int):
                self.targets[s].add(i)
        return list(w.items())

    def _commit(self, tok, reads, writes):
        for k in reads:
            self.readers.setdefault(k, {})[tok[0]] = tok
        for k in writes:
            self.lastw.setdefault(k, {})[tok[0]] = tok
            self.readers[k] = {}

    def op(self, eng, fn, reads=(), writes=(), relaxed=False):
        toks = self._deps(reads, writes)
        waits = self._waits(eng, toks, eng == "pe" or relaxed)
        idx = len(self.ops[eng])
        self.ops[eng].append(("c", fn, waits, None))
        self._commit((eng, idx), reads, writes)

    def dma(self, q, fn, reads=(), writes=()):
        k = self.dnext
        self.dnext = (k + 1) % len(self.dsem)
        toks = self._deps(reads, writes)
        if self.dcount[k] > 0:
            toks.append((k, self.dcount[k]))
        waits = self._waits(q, toks, False)
        self.dcount[k] += 1
        self.ops[q].append(("d", fn, waits, k))
        self._commit((k, self.dcount[k]), reads, writes)

    def alias(self, new, olds):
        rd = dict(self.readers.get(new, {}))
        for o_ in olds:
            toks = list(self.readers.get(o_, {}).values()) + list(self.lastw.get(o_, {}).values())
            for (s, i) in toks:
                if s not in rd or rd[s][1] < i:
                    rd[s] = (s, i)
        self.readers[new] = rd

    def final_wait(self, q, keys):
        toks = self._deps(keys, ())
        waits = self._waits(q, toks, False)
        self.ops[q].append(("w", None, waits, None))

    def emit(self):
        rank = {}
        for e in self.engs:
            rank[e] = {i: r + 1 for r, i in enumerate(sorted(self.targets[e]))}
        with self.nc.Block() as block:
            for e, attr in self.engs.items():
                ops = self.ops[e]
                if not ops:
                    continue

                def body(eng, e=e, ops=ops):
                    for idx, (kind, fn, waits, k) in enumerate(ops):
                        for (s, i) in waits:
                            if isinstance(s, int):
                                eng.wait_ge(self.dsem[s], 16 * i)
                            else:
                                eng.wait_ge(self.sem[s], rank[s][i])
                        if kind == "w":
                            continue
                        ins = fn(eng)
                        if kind == "d":
                            ins.then_inc(self.dsem[k], 16)
                        elif idx in rank[e]:
                            ins.then_inc(self.sem[e], 1)

                getattr(block, attr)(body)


def _bf16_limbs(v, n):
    v = np.asarray(v, np.float64)
    out = []
    r = v.copy()
    for _ in range(n):
        l = r.astype(np.float32).astype(NPBF)
        out.append(l)
        r = r - l.astype(np.float64)
    return out


def _alibi_slopes():
    n = 12
    s = np.exp2(-8.0 * np.arange(1, n + 1, dtype=np.float32) / n).astype(np.float32)
    return s[0::2].astype(np.float64), s[1::2].astype(np.float64)


def _make_consts():
    c = {}
    c["ident"] = np.eye(128, dtype=np.float32).astype(NPBF)
    k = np.arange(128)[:, None]
    q = np.arange(128)[None, :]
    mA = np.where(k <= q, 0.0, NEG)
    mB = np.where(k >= q, 0.0, NEG)
    mBs = np.where(k > q, 0.0, NEG)
    c["maskab"] = np.concatenate([mA, mB, mBs], 1).astype(np.float32).astype(NPBF)
    cc = np.arange(128)[:, None]
    qq = np.arange(S)[None, :]
    cm = np.where((16 * cc + 31 <= qq) & (cc < 127), 0.0, NEG)
    c["cmask"] = cm.astype(np.float32).astype(NPBF)
    dil_s, nsa_s = _alibi_slopes()
    qtab = np.zeros((12, 7, S), NPBF)
    ktab = np.zeros((5, 7, S), NPBF)

    def krows(pos):
        pos = np.asarray(pos, np.int64)
        pH = (pos // 128) * 128
        pL = pos % 128
        one = np.ones_like(pos, np.float64)
        return np.stack([one, one, one, pH, pH, pL, pL]).astype(np.float32).astype(NPBF)

    for g, r in enumerate(DIL_R):
        Lc = S // r
        a = np.tile(np.arange(Lc), r)
        ktab[g] = krows(a)
        for p in range(2):
            h = 2 * g + p
            sl = dil_s[h] * r
            hi, lo = _bf16_limbs(np.full(S, sl), 2)
            L = _bf16_limbs(-sl * a.astype(np.float64), 3)
            qtab[h] = np.stack([L[0], L[1], L[2], hi, lo, hi, lo])
    pos = np.arange(S)
    ktab[3] = krows(pos)
    ce = np.zeros(S, np.int64)
    ce[:127] = 16 * np.arange(127) + 31
    ktab[4] = krows(ce)
    for h in range(6):
        sl = nsa_s[h]
        hi, lo = _bf16_limbs(np.full(S, sl), 2)
        L = _bf16_limbs(-sl * pos.astype(np.float64), 3)
        qtab[6 + h] = np.stack([L[0], L[1], L[2], hi, lo, hi, lo])
    c["qtab"] = qtab
    c["ktab"] = ktab
    j = np.arange(32)[:, None]
    kk = np.arange(S)[None, :]
    c["ksel"] = (kk // 64 == j).astype(np.float32).astype(NPBF)
    qpos = (np.arange(16)[None, :, None] * 128 + np.arange(128)[:, None, None])
    jj = np.arange(32)[None, None, :]
    causal = (jj * 64 <= qpos)
    qblk = qpos // 64
    forced = (jj == 0) | (jj == qblk) | (jj == qblk - 1)
    m1 = (causal & ~forced).astype(np.float32)
    m2 = np.where(causal & forced, 1e6, np.where(causal, 0.0, -1.0)).astype(np.float32)
    c["m1"] = m1.reshape(128, 512)
    c["m2"] = m2.reshape(128, 512)
    cidx = np.arange(128)[:, None]
    cstart = 16 * cidx
    cend = cstart + 31
    sst = np.arange(32)[None, :] * 64
    ov = ((cstart < sst + 64) & (cend >= sst) & (cidx < 127)).astype(np.float32)
    ov33 = np.concatenate([ov, (cidx < 127).astype(np.float32)], 1)
    c["ov33"] = ov33.astype(NPBF)
    sg = np.zeros((9, 9, 128), np.float32)
    for k_ in range(9):
        sg[k_, k_, :] = 1.0
    c["selg"] = sg.astype(NPBF)
    return c


CONST_SPECS = [
    ("ident", [128, 128], BF16), ("maskab", [128, 384], BF16), ("cmask", [128, S], BF16),
    ("qtab", [12, 7, S], BF16), ("ktab", [5, 7, S], BF16), ("ksel", [32, S], BF16),
    ("m1", [128, 512], F32), ("m2", [128, 512], F32), ("ov33", [128, 33], BF16), ("selg", [9, 9, 128], BF16),
]

WEIGHTS = [
    ("w_in", [D, DIN]), ("w_pa", [256, D]), ("w_pb", [384, D]), ("w_pc", [384, D]), ("w_o", [D, D]),
    ("w_up", [D, 2 * DFF]), ("w_down", [DFF, D]),
    ("cmp_k_w1", [2048, 256]), ("cmp_k_w2", [256, 64]), ("cmp_v_w1", [2048, 256]), ("cmp_v_w2", [256, 64]),
]

def _bias_blocks():
    bl = []
    for p in range(2):
        bl.append(("qa_p%d" % p, [(0, C_QA + 128 * p, 128)]))
        bl.append(("ka_p%d" % p, [(0, C_KA + 128 * p, 128)]))
    bl.append(("fa", [(0, C_FA, 4)]))
    for g in range(3):
        bl.append(("qb_g%d" % g, [(0, C_QB + 128 * g, 128)]))
        bl.append(("kb_g%d" % g, [(0, C_KB + 128 * g, 128)]))
        bl.append(("vb_g%d" % g, [(0, C_VB + 128 * g, 128)]))
    for g in range(2):
        bl.append(("n0_g%d" % g, [(0, C_QC + 192 * g, 128)]))
        bl.append(("n1_g%d" % g, [(0, C_QC + 192 * g + 128, 64), (64, C_KSL + 64 * g, 64)]))
        bl.append(("n2_g%d" % g, [(0, C_KWN + 64 * g, 64), (64, C_KC + 64 * g, 64)]))
        bl.append(("n3_g%d" % g, [(0, C_VC + 64 * g, 64)]))
        bl.append(("gc%d" % g, [(64, C_GC + 9 * g, 9)]))
    for i in range(24):
        bl.append(("gt%d" % i, [(0, C_GATES + 128 * i, 128)]))
    return bl


BIAS_BLOCKS = _bias_blocks()
BIDX = {n: i for i, (n, _) in enumerate(BIAS_BLOCKS)}
NB = len(BIAS_BLOCKS)
VROW_COLS = (list(range(C_VA, C_VA + 256)) + list(range(C_VSL, C_VSL + 64)) + list(range(C_VWN, C_VWN + 64))
             + list(range(C_VSL + 64, C_VSL + 128)) + list(range(C_VWN + 64, C_VWN + 128)))


def build_program(nseq, depth, enable=("fox", "dil", "nsa"), debug=None, stop_after=None):
    nc = bass.Bass("TRN2", target_bir_lowering=False)
    es = ExitStack()
    dr = {}

    def din(name, shape, dt):
        dr[name] = nc.dram_tensor(name, list(shape), dt, kind="ExternalInput").ap()
        return dr[name]

    x_in = din("x", [nseq, S, D], F32)
    for n, shp in WEIGHTS:
        din(n, [depth] + shp, F32)
    din("ngb", [2 * depth + 1, 128, D], F32)
    din("ngt", [2 * depth + 1, 128, D], F32)
    din("bcol", [depth, 128, NB], F32)
    din("vrow", [depth, 1, 512], F32)
    din("cw", [depth, 128, FC * 4], F32)
    din("pet", [depth, 64, 32], F32)
    for n, shp, dt in CONST_SPECS:
        din(n, shp, dt)
    out = nc.dram_tensor("out", [nseq, S, D], F32, kind="ExternalOutput").ap()
    wb = {n: nc.dram_tensor("wb_" + n, [depth] + shp, BF16, kind="Internal").ap() for n, shp in WEIGHTS}
    skind = "ExternalOutput" if debug else "Internal"
    wg = nc.dram_tensor("wg", [depth, 8, 128, 3072], BF16, kind="Internal").ap()
    wu = nc.dram_tensor("wu", [depth, 11, 128, 4096], BF16, kind="Internal").ap()
    wd = nc.dram_tensor("wd", [depth, 4, 128, 5632], BF16, kind="Internal").ap()
    xres = nc.dram_tensor("xres", [S, D], F32, kind=skind).ap()
    hTd = nc.dram_tensor("hTd", [128, DC, S], BF16, kind=skind).ap()
    ytd = nc.dram_tensor("ytd", [128, 8, S], BF16, kind=skind).ap()
    dbg = None

    sc = Sched(nc, es)

    def sb(name, shape, dt):
        return es.enter_context(nc.sbuf_tensor(name, list(shape), dt))

    def ps(name, shape, dt=F32):
        return es.enter_context(nc.psum_tensor(name, list(shape), dt))

    NAR = 8192 + 16384 + 16384 + 6144 + 9600 + 5632 + 5632
    AR = sb("arena", [128, NAR], BF16)
    o = 0
    HS = [AR[:, o + i * 4096:o + (i + 1) * 4096].rearrange("p (c t) -> p c t", c=DC) for i in range(2)]
    o += 8192
    YT = AR[:, o:o + 16384].rearrange("p (c t) -> p c t", c=8)
    o += 16384
    QK = [AR[:, o + i * 2048:o + (i + 1) * 2048] for i in range(8)]
    QKALL = AR[:, o:o + 16384]
    o += 16384
    VV = AR[:, o:o + 6144].rearrange("p (t pr x) -> p t pr x", t=16, pr=2)
    o += 6144
    WM = AR[:, o:o + 9600]
    o += 9600
    W01 = [AR[:, o + i * 5632:o + (i + 1) * 5632] for i in range(2)]
    W01ALL = AR[:, o:o + 11264]
    gs_ctr = [0]
    o += 11264
    FA = sb("f32a", [128, 3072], F32)
    XT = [FA[:, i * 1024:(i + 1) * 1024] for i in range(3)]
    GBC = sb("gbc", [128, D], F32)
    HN = [sb("hn%d" % i, [128, D], BF16) for i in range(2)]
    SS = sb("ss", [128, 32], F32)
    IDENT = sb("ident_s", [128, 128], BF16)
    MASKAB = sb("maskab_s", [128, 384], BF16)
    CMASK = sb("cmask_s", [128, S], BF16)
    M1 = sb("m1_s", [128, 512], F32)
    M2 = sb("m2_s", [128, 512], F32)
    OV33 = sb("ov33_s", [128, 33], BF16)
    BCOL = sb("bcol_s", [128, NB], F32)
    VROW = sb("vrow_s", [1, 512], BF16)
    ONESB = sb("onesb", [128, 128], BF16)
    ZEROB = sb("zerob", [128, 128], BF16)
    PT = [sb("pt%d" % i, [128, 512], BF16) for i in range(3)]
    RB = [sb("rb%d" % i, [128, 512], F32) for i in range(2)]
    T1 = sb("t1", [128, 512], F32)
    FL = T1[0:4, :]
    LS = sb("ls", [128, S], F32)
    CSP = sb("csp", [4, 1], F32)
    CK = sb("ck", [4, 2, 512], BF16)
    CQ = sb("cq", [4, 2, 512], BF16)
    TMPB = [sb("tmpb%d" % i, [128, 512], BF16) for i in range(3)]
    KCT = sb("kct", [128, 128], BF16)
    VC = sb("vc", [128, 192], BF16)
    HG = [sb("hg%d" % i, [128, 2, 128], BF16) for i in range(2)]
    HB = sb("hb", [128, 4], F32)
    W2T = sb("w2t", [128, 2, 2, 64], BF16)
    PETF = sb("petf", [64, 32], F32)
    PETB = sb("petb", [64, 32], BF16)
    IMPS = sb("imps", [128, 512], F32)
    SCR = sb("scr", [128, 512], F32)
    CS = SCR[0:4, :]
    MX = sb("mx", [128, 16, 8], F32)
    RI = sb("ri", [128, 4], F32)
    NSEL = sb("nsel", [128, 512], BF16)
    CW = sb("cw_s", [128, FC * 4], F32)
    AH = sb("ah", [128, FC, 2], F32)
    GSB = sb("gsb", [128, S], BF16)
    SELG = sb("selg_s", [128, 9, 128], BF16)

    PS_S = [ps("ps_s%d" % i, [128, 512]) for i in range(3)]
    PS_A = [ps("ps_a%d" % i, [128, 512]) for i in range(2)]
    PS_B = ps("ps_b", [128, 512])
    PS_P = [ps("ps_p%d" % i, [128, 512]) for i in range(2)]
    LNB = sb("lnb", [128, 1], F32)

    K = lambda *a: a

    def mm(out_, lhsT, rhs, start, stop, reads, writes):
        sc.op("pe", lambda e: e.matmul(out_, lhsT=lhsT, rhs=rhs, start=start, stop=stop, skip_group_check=True),
              reads, writes)

    def act(out_, in_, func, reads, writes, bias=None, scale=None, accum_out=None):
        kw = {}
        if bias is not None:
            kw["bias"] = bias
        if scale is not None:
            kw["scale"] = scale
        if accum_out is not None:
            kw["accum_out"] = accum_out
        sc.op("act", lambda e: e.activation(out=out_, in_=in_, func=func, **kw), reads, writes)

    def ts(eng, out_, in0, s1, s2, op0, op1, reads, writes):
        if op1 is None:
            sc.op(eng, lambda e: e.tensor_scalar(out=out_, in0=in0, scalar1=s1, scalar2=None, op0=op0), reads, writes)
        else:
            sc.op(eng, lambda e: e.tensor_scalar(out=out_, in0=in0, scalar1=s1, scalar2=s2, op0=op0, op1=op1),
                  reads, writes)

    def tt(eng, out_, in0, in1, op, reads, writes):
        sc.op(eng, lambda e: e.tensor_tensor(out=out_, in0=in0, in1=in1, op=op), reads, writes)

    def stt(out_, in0, scalar, in1, op0, op1, reads, writes, relaxed=False):
        sc.op("dve", lambda e: e.scalar_tensor_tensor(out=out_, in0=in0, scalar=scalar, in1=in1, op0=op0, op1=op1),
              reads, writes, relaxed=relaxed)

    def cp(eng, out_, in_, reads, writes):
        if eng == "act":
            sc.op("act", lambda e: e.copy(out=out_, in_=in_), reads, writes)
        else:
            sc.op(eng, lambda e: e.tensor_copy(out=out_, in_=in_), reads, writes)

    def dma(q, out_, in_, reads, writes):
        sc.dma(q, lambda e: e.dma_start(out=out_, in_=in_), reads, writes)

    def memset(eng, ap, val, writes):
        sc.op(eng, lambda e: e.memset(ap, val), (), writes)

    dma("sp", IDENT[:, :], dr["ident"][:, :], (), ["ident"])
    dma("sp", MASKAB[:, :], dr["maskab"][:, :], (), ["maskab"])
    dma("sp", CMASK[:, :], dr["cmask"][:, :], (), ["cmask"])
    dma("sp", M1[:, :], dr["m1"][:, :], (), ["m1"])
    dma("sp", M2[:, :], dr["m2"][:, :], (), ["m2"])
    dma("sp", OV33[:, :], dr["ov33"][:, :], (), ["ov33"])
    memset("pool", ONESB[:, :], 1.0, ["onesb"])
    memset("pool", ZEROB[:, :], 0.0, ["zerob"])
    memset("pool", LNB[:, :], 1e-18, ["lnb"])
    memset("pool", QKALL, 0.0, ["qk%d" % i for i in range(8)])
    memset("pool", VV[:, :, :, 64:128], 1.0, ["vv"])
    memset("pool", VC[:, :], 0.0, ["vc"])
    memset("pool", VC[:, 64:128], 1.0, ["vc"])
    memset("pool", SELG[:, :, :], 0.0, ["selg"])
    memset("pool", GSB[:, :], 0.0, ["gsb"])
    dma("sp", SELG[64:73, :, :], dr["selg"][:, :, :], (), ["selg"])
    memset("pool", KCT[:, :], 0.0, ["kct"])
    memset("pool", AH[:, :, :], 0.0, ["ah"])

    conv_order = ["w_in", "cmp_k_w1", "cmp_k_w2", "cmp_v_w1", "cmp_v_w2", "w_pa", "w_pb", "w_pc", "w_o", "w_up", "w_down"]
    for l in range(depth):
        for n in conv_order:
            shp = dict(WEIGHTS)[n]
            rows = shp[0]
            step = 512 if rows > 512 else rows
            for r0 in range(0, rows, step):
                r1 = min(rows, r0 + step)
                dma("pool", wb[n][l, r0:r1, :], dr[n][l, r0:r1, :], (), [K("wb", n, l)])

    def relayout(l):
        for blk in range(24):
            c0 = C_GATES + blk * 128
            b_, c_ = divmod(blk, 8)
            dma("sp", wg[l, c_][:, b_ * 1024:(b_ + 1) * 1024].rearrange("p (c n) -> p c n", c=DC), wb["w_in"][l, :, c0:c0 + 128].rearrange("(c p) n -> p c n", p=128),
                [K("wb", "w_in", l)], [K("wg", l)])
        for fp in range(FC // 2):
            for ab in range(2):
                c0 = ab * DFF + fp * 256
                dma("sp", wu[l, fp].rearrange("p (c b n) -> p c b n", c=DC, b=2)[:, :, ab, :],
                    wb["w_up"][l, :, c0:c0 + 256].rearrange("(c p) n -> p c n", p=128), [K("wb", "w_up", l)], [K("wu", l)])
        for qd in range(4):
            dma("sp", wd[l, qd].rearrange("p (c n) -> p c n", c=FC), wb["w_down"][l, :, qd * 256:(qd + 1) * 256].rearrange("(c p) n -> p c n", p=128),
                [K("wb", "w_down", l)], [K("wd", l)])

    def load_layer_small(l):
        dma("sp", BCOL[:, :], dr["bcol"][l], (), ["bcol"])
        dma("pool", VROW[:, :], dr["vrow"][l], (), ["vrow"])
        dma("sp", CW[:, :], dr["cw"][l], (), ["cw"])
        dma("sp", PETF[:, :], dr["pet"][l], (), ["petf"])
        cp("dve", PETB[:, :], PETF[:, :], ["petf"], ["petb"])
        for kv, n in enumerate(("cmp_k_w2", "cmp_v_w2")):
            dma("sp", W2T[:, kv, :, :], wb[n][l].rearrange("(c p) n -> p c n", p=128), [K("wb", n, l)], ["w2t"])

    FAKEYS = ["xt0", "xt1", "xt2", "gt0", "gt1", "gt2", "rx0", "rx1", "rx2", "ab0", "ab1", "tb0", "tb1"]

    def fa_alias(keys):
        for k_ in keys:
            sc.alias(k_, [o_ for o_ in FAKEYS if o_ != k_])

    def xkeys(t):
        return [("xres", t, p_) for p_ in ("h0", "h1", "q0", "q1", "q2", "q3")]

    def norm_phase(xsrc, gidx, to_dram=True, final_out=None):
        fa_alias(["xt0", "xt1", "xt2"])
        if final_out is not None:
            dma("sp", GBC[:, :], dr["ngb"][gidx], (), ["gbc"])
        else:
            dma("sp", GBC[:, :], dr["ngt"][gidx], (), ["gbc"])
        gbt = GBC[:, :].rearrange("p (c t) -> p c t", c=DC)

        def stL(t):
            dma("sp", XT[t % 3], xsrc[t * 128:(t + 1) * 128, :], xkeys(t), ["xt%d" % (t % 3)])

        SQB = [(SCR[:, :].bitcast(BF16), "scr"), (T1[:, :].bitcast(BF16), "t1")]

        def stA(t):
            xt, kx = XT[t % 3], "xt%d" % (t % 3)
            sq, ksq = SQB[t % 2]
            act(sq, xt, AF.Square, [kx], [ksq])
            sc.op("dve", lambda e, sq=sq, t=t: e.reduce_sum(out=SS[:, t:t + 1], in_=sq, axis=mybir.AxisListType.X), [ksq], [("ss", t)])

        def stB1(t):
            act(SS[:, 16 + t:17 + t], SS[:, t:t + 1], AF.Sqrt, [("ss", t)], [("rs", t)], bias=1e-6, scale=1.0 / D)
            sc.op("dve", lambda e, t=t: e.reciprocal(out=SS[:, 16 + t:17 + t], in_=SS[:, 16 + t:17 + t]), [("rs", t)], [("rs", t)])

        def stB(t):
            xt, kx = XT[t % 3], "xt%d" % (t % 3)
            hn, kh = HN[t % 2], "hn%d" % (t % 2)
            if final_out is not None:
                stt(xt, xt, SS[:, 16 + t:17 + t], GBC[:, :], ALU.mult, ALU.mult, [kx, ("rs", t), "gbc"], [kx])
                dma("pool", final_out[t * 128:(t + 1) * 128, :], xt, [kx], ["outd"])
                return
            act(hn[:, :], xt, AF.Copy, [kx, ("rs", t)], [kh], scale=SS[:, 16 + t:17 + t])
            pp = PS_P[t % 2]
            kp = "ps_p%d" % (t % 2)
            ppb = pp[:, :].bitcast(BF16)
            for c in range(DC):
                sc.op("pe", lambda e, c=c, ppb=ppb, hn=hn: e.transpose(ppb[:, c * 128:(c + 1) * 128], hn[:, c * 128:(c + 1) * 128], IDENT[:, :]),
                      [kh, "ident"], [kp])
            tg, t4 = divmod(t, 4)
            hs = HS[tg % 2]
            khs = "hs%d" % (tg % 2)
            tt("dve", hs[:, :, t4 * 128:(t4 + 1) * 128], ppb.rearrange("p (c t) -> p c t", c=DC), gbt, ALU.mult, [kp, "gbc"], [khs])
            if t4 == 3:
                dma("pool", hTd[:, :, tg * 512:(tg + 1) * 512], hs, [khs], ["hTd"])

        stL(0)
        stL(1)
        stA(0)
        for t in range(NT):
            if t + 2 < NT:
                stL(t + 2)
            stB1(t)
            if t + 1 < NT:
                stA(t + 1)
            stB(t)

    def load_hs(tg, i):
        dma("sp", HS[i], hTd[:, :, tg * 512:(tg + 1) * 512], ["hTd"], ["hs%d" % i])

    hs_ctr = [0]

    def next_hs(tg):
        i = hs_ctr[0] % 2
        hs_ctr[0] += 1
        load_hs(tg, i)
        return HS[i], "hs%d" % i

    pp_ctr = [0]
    pf_ctr = [0]
    PBANKS = [(PS_P[0], "ps_p0"), (PS_P[1], "ps_p1"), (PS_S[0], "ps_s0"), (PS_S[1], "ps_s1"), (PS_S[2], "ps_s2")]

    def proj_fm(hs, khs, wv, c0, m, kw, evac):
        banks = [(PS_P[0], "ps_p0"), (PS_P[1], "ps_p1"), (PS_S[0], "ps_s0"), (PS_S[1], "ps_s1"), (PS_S[2], "ps_s2")]
        pp, kp = banks[pf_ctr[0] % 5]
        pf_ctr[0] += 1
        for dc in range(DC):
            mm(pp[0:m, :], wv[:, dc, c0:c0 + m], hs[:, dc, :], dc == 0, dc == DC - 1, [khs, kw], [kp])
        evac(pp, kp)

    def bias_ap(name, m, p0=0):
        return BCOL[p0:p0 + m, BIDX[name]:BIDX[name] + 1]

    st_ctr = [0]
    acc_ctr = [0]

    pending_fin = [None]

    def flush_fin():
        f = pending_fin[0]
        pending_fin[0] = None
        if f is not None:
            f()

    def defer_fin(f):
        flush_fin()
        pending_fin[0] = f

    def attn_group(steps, qt, kq, kt, kk, vfn):
        a = acc_ctr[0] % 2
        acc_ctr[0] += 1
        acc = PS_A[a]
        ka = "ps_a%d" % a
        if ZERO_INIT:
            mm(acc[:, :], ZEROB[:, :], CMASK[:, 0:512], True, False, ["zerob", "cmask"], [ka])
        pendq = []
        n = len(steps)
        for si, (kc0, q0, nq, mask) in enumerate(steps):
            s = st_ctr[0] % 3
            st_ctr[0] += 1
            pss = PS_S[s]
            ks = "ps_s%d" % s
            mm(pss[:, 0:nq], kt[:, kc0:kc0 + 128], qt[:, q0:q0 + nq], True, mask is None, [kk, kq], [ks])
            if mask is not None:
                map_, moff, mkey = mask
                w = map_.shape[1]
                mm(pss[:, moff:moff + w], IDENT[:, :], map_, False, True, ["ident", mkey], [ks])
            pt = PT[s]
            kpt = "pt%d" % s
            act(pt[:, 0:nq], pss[:, 0:nq], AF.Exp, [ks], [kpt])
            if si == min(1, n - 1):
                flush_fin()
            if len(pendq) >= 2:
                pendq.pop(0)()
            def pv(si=si, pt=pt, kpt=kpt, q0=q0, nq=nq, kc0=kc0):
                vl, kv = vfn(kc0)
                first = (si == 0) and not ZERO_INIT
                mm(acc[:, q0 % 512:q0 % 512 + nq], vl, pt[:, 0:nq], first, si == n - 1, [kpt, kv], [ka])
            pendq.append(pv)
        for f_ in pendq:
            f_()
        return acc, ka

    rb_ctr = [0]

    def recip_rows(src_ap, ksrc, pb, spb):
        i = rb_ctr[0] % 2
        rb_ctr[0] += 1
        rb = RB[i][pb:pb + 64, :]
        act(rb, src_ap, AF.Ln, [ksrc, "lnb"], ["rb%d" % i], bias=LNB[spb:spb + 64, 0:1])
        act(rb, rb, AF.Exp, ["rb%d" % i], ["rb%d" % i], scale=-1.0)
        return rb, "rb%d" % i

    def finalize_simple(acc, ka, pb, dst, kdst_w):
        rb, krb = recip_rows(acc[64 - pb:128 - pb, :], ka, pb, 64 - pb)
        tt("dve", dst, acc[pb:pb + 64, :], rb, ALU.mult, [ka, krb], [kdst_w])

    def ytile(h_global):
        return h_global // 2, (h_global % 2) * 64

    def fox_phase(l):
        wv = WM[:, 0:DC * 772].rearrange("p (c n) -> p c n", c=DC)
        dma("sp", wv, wb["w_in"][l, :, 0:772].rearrange("(c p) n -> p c n", p=128), [K("wb", "w_in", l)], ["wm"])
        for h in range(8):
            memset("pool", QK[h][64:128, :], 0.0, ["qk%d" % h])
            memset("pool", QK[h][64:68, :], 1.0, ["qk%d" % h])
        for tg in range(NTG):
            hs, khs = next_hs(tg)
            cs_ = slice(tg * 512, (tg + 1) * 512)
            for pr in range(2):
                def evq(pp, kp, pr=pr):
                    for hh in range(2):
                        h = 2 * pr + hh
                        ts("dve", QK[h][0:64, cs_], pp[64 * hh:64 * hh + 64, :], bias_ap("qa_p%d" % pr, 64, 64 * hh), 0.125, ALU.add, ALU.mult,
                           [kp, "bcol"], ["qk%d" % h])
                proj_fm(hs, khs, wv, C_QA + 128 * pr, 128, "wm", evq)
                def evk(pp, kp, pr=pr):
                    for hh in range(2):
                        h = 2 * pr + hh
                        act(QK[4 + h][0:64, cs_], pp[64 * hh:64 * hh + 64, :], AF.Identity, [kp, "bcol"], ["qk%d" % (4 + h)],
                            bias=bias_ap("ka_p%d" % pr, 64, 64 * hh))
                proj_fm(hs, khs, wv, C_KA + 128 * pr, 128, "wm", evk)
            def evf(pp, kp):
                ts("dve", FL, pp[0:4, :], bias_ap("fa", 4), -1.0, ALU.add, ALU.mult, [kp, "bcol"], ["t1"])
                act(FL, FL, AF.Exp, ["t1"], ["t1"])
                act(FL, FL, AF.Ln, ["t1"], ["t1"], bias=1.0)
            proj_fm(hs, khs, wv, C_FA, 4, "wm", evf)
            init = 0.0 if tg == 0 else CSP[:, 0:1]
            sc.op("dve", lambda e, init=init: e.tensor_tensor_scan(out=CS, data0=FL, data1=FL, initial=init,
                                                                    op0=ALU.add, op1=ALU.max), ["t1", "csp"], ["scr"])
            cp("dve", CSP[:, 0:1], SCR[0:4, 511:512], ["scr"], ["csp"])
            cp("dve", CK[:, 0, :], CS, ["scr"], ["ck"])
            tt("dve", CK[:, 1, :], CS, CK[:, 0, :], ALU.subtract, ["scr", "ck"], ["ck"])
            ts("dve", CQ[:, :, :], CK[:, :, :], -1.0, None, ALU.mult, None, ["ck"], ["cq"])
            for h in range(4):
                dma("pool", QK[h][64:66, cs_], CQ[h:h + 1, :, :], ["cq"], ["qk%d" % h])
                dma("pool", QK[4 + h][66:68, cs_], CK[h:h + 1, :, :], ["ck"], ["qk%d" % (4 + h)])
            for t4 in range(4):
                t = tg * 4 + t4
                pp, kp = PBANKS[pf_ctr[0] % 5]
                pf_ctr[0] += 1
                for dc in range(DC):
                    mm(pp[:, 0:256], hs[:, dc, t4 * 128:(t4 + 1) * 128], wv[:, dc, C_VA:C_VA + 256], dc == 0, False, [khs, "wm"], [kp])
                mm(pp[:, 0:256], ONESB[0:1, 0:128], VROW[0:1, 0:256], False, True, ["onesb", "vrow"], [kp])
                cp("act", VV[:, t, :, :].rearrange("p pr (j e) -> p pr j e", j=3)[:, :, 0:3:2, :],
                   pp[:, 0:256].rearrange("p (pr j e) -> p pr j e", pr=2, j=2), [kp], ["vv"])
        for h in range(4):
            ch, pb = ytile(h)
            for G in range(NTG):
                steps = []
                for j in range(4 * G + 4):
                    m = max(0, j - 4 * G)
                    q0 = (4 * G + m) * 128
                    nq = 512 - 128 * m
                    mask = (MASKAB[:, 0:128], 0, "maskab") if j >= 4 * G else None
                    steps.append((j * 128, q0, nq, mask))
                acc, ka = attn_group(steps, QK[h], "qk%d" % h, QK[4 + h], "qk%d" % (4 + h),
                                     lambda kc0, h=h: (VV[:, kc0 // 128, h // 2, (h % 2) * 64:(h % 2) * 64 + 128], "vv"))
                defer_fin(lambda acc=acc, ka=ka, pb=pb, ch=ch, G=G: finalize_simple(acc, ka, pb, YT[pb:pb + 64, ch, G * 512:(G + 1) * 512], "yt"))
        flush_fin()

    def dil_phase(l):
        for g, r in enumerate(DIL_R):
            Lc = S // r
            nbc = Lc // 128
            wv = WM[:, 0:DC * 384].rearrange("p (c n) -> p c n", c=DC)
            for j, c0 in enumerate((C_QB, C_KB, C_VB)):
                dma("sp", wv[:, :, j * 128:(j + 1) * 128],
                    wb["w_in"][l, :, c0 + 128 * g:c0 + 128 * g + 128].rearrange("(c p) n -> p c n", p=128),
                    [K("wb", "w_in", l)], ["wm"])
            for i_ in (0, 1, 4, 5):
                memset("pool", QK[i_][64:128, :], 0.0, ["qk%d" % i_])
            for p in range(2):
                dma("sp", QK[p][64:71, :], dr["qtab"][2 * g + p], (), ["qk%d" % p])
                dma("sp", QK[4 + p][64:71, :], dr["ktab"][g], (), ["qk%d" % (4 + p)])
            for tg in range(NTG):
                hs, khs = next_hs(tg)
                for j, (nm, dbase) in enumerate((("qb", 0), ("kb", 4), ("vb", 2))):
                    def ev(pp, kp, nm=nm, dbase=dbase):
                        for p in range(2):
                            dsti = dbase + p
                            dst = QK[dsti][0:64, :].rearrange("p (r a) -> p a r", r=r)[:, tg * (512 // r):(tg + 1) * (512 // r), :]
                            src_ = pp[64 * p:64 * p + 64, :].rearrange("p (a r) -> p a r", r=r)
                            if nm == "qb":
                                ts("dve", dst, src_, bias_ap("qb_g%d" % g, 64, 64 * p), 0.125, ALU.add, ALU.mult, [kp, "bcol"], ["qk%d" % dsti])
                            else:
                                act(dst, src_, AF.Identity, [kp, "bcol"], ["qk%d" % dsti], bias=bias_ap("%s_g%d" % (nm, g), 64, 64 * p))
                    proj_fm(hs, khs, wv, j * 128, 128, "wm", ev)
            for p in range(2):
                for ub in range(16):
                    pp_, kp = PBANKS[pf_ctr[0] % 5]
                    pf_ctr[0] += 1
                    ppb = pp_[:, :].bitcast(BF16)
                    sc.op("pe", lambda e, ppb=ppb, p=p, ub=ub: e.transpose(ppb[:, 0:64], QK[2 + p][0:64, ub * 128:(ub + 1) * 128], IDENT[0:64, 0:64]),
                          ["qk%d" % (2 + p), "ident"], [kp])
                    cp("act", VV[:, ub, 0, p * 128:p * 128 + 64], ppb[:, 0:64], [kp], ["vv"])
            for p in range(2):
                h = 2 * g + p
                ch, pb = ytile(4 + h)
                for Gq in range(NTG):
                    steps = []
                    for kb in range(4 * Gq - 1, 4 * Gq + 4):
                        if kb < 0:
                            continue
                        ql = [q for q in (kb, kb + 1) if 4 * Gq <= q < 4 * Gq + 4 and q // nbc == kb // nbc]
                        if not ql:
                            continue
                        if ql == [kb, kb + 1]:
                            mask = (MASKAB[:, 0:256], 0, "maskab")
                        elif ql == [kb]:
                            mask = (MASKAB[:, 0:128], 0, "maskab")
                        else:
                            mask = (MASKAB[:, 128:256], 0, "maskab")
                        steps.append((kb * 128, ql[0] * 128, 128 * len(ql), mask))
                    acc, ka = attn_group(steps, QK[p], "qk%d" % p, QK[4 + p], "qk%d" % (4 + p),
                                         lambda kc0, p=p: (VV[:, kc0 // 128, 0, p * 64:p * 64 + 128], "vv"))
                    def dil_fin(acc=acc, ka=ka, Gq=Gq, p=p, g=g, r=r, pb=pb, ch=ch):
                        if r == 1:
                            r0, nr, a0, na = 0, 1, 512 * Gq, 512
                        elif r == 4:
                            r0, nr, a0, na = Gq, 1, 0, 512
                        else:
                            r0, nr, a0, na = 4 * Gq, 4, 0, 128
                        lp = 64 - pb
                        dsty = YT[pb:pb + 64, ch, :].rearrange("p (a r) -> p r a", r=r)[:, r0:r0 + nr, a0:a0 + na]
                        cp("act", dsty, acc[pb:pb + 64, :].rearrange("p (r a) -> p r a", r=nr), [ka], ["yt", "accsync"])
                        dstl = LS[lp:lp + 64, :].rearrange("p (a r) -> p r a", r=r)[:, r0:r0 + nr, a0:a0 + na]
                        srcl = acc[lp:lp + 64, :].rearrange("p (r a) -> p r a", r=nr)
                        if g == 0:
                            cp("dve", dstl, srcl, [ka, "accsync"], ["ls%d" % p])
                        else:
                            tt("dve", dstl, dstl, srcl, ALU.add, [ka, "accsync", "ls%d" % p], ["ls%d" % p])
                    defer_fin(dil_fin)
        flush_fin()
        for p in range(2):
            pb = 64 * p
            lp = 64 - pb
            for tg in range(NTG):
                cs_ = slice(tg * 512, (tg + 1) * 512)
                rb, krb = recip_rows(LS[lp:lp + 64, cs_], "ls%d" % p, pb, lp)
                for g in range(3):
                    ch, _pb = ytile(4 + 2 * g + p)
                    yv = YT[pb:pb + 64, ch, cs_]
                    tt("dve", yv, yv, rb, ALU.mult, ["yt", krb], ["yt"])

    def nsa_phase(l):
        NWC = 192 + 6 * 64 + 9
        OQ, OKSL, OKWN, OKC, OVC, OVSL, OVWN, OGC = 0, 192, 256, 320, 384, 448, 512, 576
        for g in range(2):
            wv = WM[:, 0:DC * NWC].rearrange("p (c n) -> p c n", c=DC)
            segs = [(OQ, C_QC + 192 * g, 192), (OKSL, C_KSL + 64 * g, 64), (OKWN, C_KWN + 64 * g, 64), (OKC, C_KC + 64 * g, 64),
                    (OVC, C_VC + 64 * g, 64), (OVSL, C_VSL + 64 * g, 64), (OVWN, C_VWN + 64 * g, 64), (OGC, C_GC + 9 * g, 9)]
            for (o_, c0, n_) in segs:
                dma("sp", wv[:, :, o_:o_ + n_], wb["w_in"][l, :, c0:c0 + n_].rearrange("(c p) n -> p c n", p=128),
                    [K("wb", "w_in", l)], ["wm"])
            for i in (0, 1, 2, 4, 5):
                memset("pool", QK[i][64:128, :], 0.0, ["qk%d" % i])
            for i in range(3):
                dma("sp", QK[i][64:71, :], dr["qtab"][6 + 3 * g + i], (), ["qk%d" % i])
            dma("sp", QK[4][64:71, :], dr["ktab"][3], (), ["qk4"])
            dma("sp", QK[5][64:71, :], dr["ktab"][3], (), ["qk5"])
            dma("sp", QK[4][96:128, :], dr["ksel"][:, :], (), ["qk4"])
            dma("sp", KCT[64:71, :], dr["ktab"][4, :, 0:128], (), ["kct"])
            for tg in range(NTG):
                hs, khs = next_hs(tg)
                cs_ = slice(tg * 512, (tg + 1) * 512)
                plan = [("n0", [(0, 0, True), (1, 1, True)]), ("n1", [(0, 2, True), (1, 4, False)]),
                        ("n2", [(0, 5, False), (1, 3, False)]), ("n3", [(0, 6, False)])]
                for gi, (bn, parts) in enumerate(plan):
                    def evn(pp, kp, bn=bn, parts=parts):
                        for (hh, dsti, isq) in parts:
                            if isq:
                                ts("dve", QK[dsti][0:64, cs_], pp[64 * hh:64 * hh + 64, :], bias_ap("%s_g%d" % (bn, g), 64, 64 * hh), 0.125,
                                   ALU.add, ALU.mult, [kp, "bcol"], ["qk%d" % dsti])
                            elif any(p_[2] for p_ in parts):
                                ts("dve", QK[dsti][0:64, cs_], pp[64 * hh:64 * hh + 64, :], bias_ap("%s_g%d" % (bn, g), 64, 64 * hh), None,
                                   ALU.add, None, [kp, "bcol"], ["qk%d" % dsti])
                            else:
                                act(QK[dsti][0:64, cs_], pp[64 * hh:64 * hh + 64, :], AF.Identity, [kp, "bcol"], ["qk%d" % dsti],
                                    bias=bias_ap("%s_g%d" % (bn, g), 64, 64 * hh))
                    proj_fm(hs, khs, wv, 128 * gi, 128, "wm", evn)
                def evg(pp, kp):
                    act(GSB[64:73, cs_], pp[64:73, :], AF.Sigmoid, [kp, "bcol"], ["gsb"], bias=bias_ap("gc%d" % g, 9, 64))
                proj_fm(hs, khs, wv, OGC - 64, 73, "wm", evg)
                for t4 in range(4):
                    t = tg * 4 + t4
                    pp, kp = PBANKS[pf_ctr[0] % 5]
                    pf_ctr[0] += 1
                    for dc in range(DC):
                        mm(pp[:, 0:128], hs[:, dc, t4 * 128:(t4 + 1) * 128], wv[:, dc, OVSL:OVSL + 128], dc == 0, False, [khs, "wm"], [kp])
                    mm(pp[:, 0:128], ONESB[0:1, 0:128], VROW[0:1, 256 + 128 * g:384 + 128 * g], False, True, ["onesb", "vrow"], [kp])
                    vv3 = VV[:, t, :, :].rearrange("p pr (j e) -> p pr j e", j=3)
                    cp("act", vv3[:, :, 0, :], pp[:, 0:128].rearrange("p (h e) -> p h e", h=2), [kp], ["vv"])
                    cp("pool", vv3[:, :, 2, :], vv3[:, :, 0, :], ["vv"], ["vv"])
            for kv, (src, wn) in enumerate(((3, "cmp_k_w1"), (6, "cmp_v_w1"))):
                srcT = QK[src]
                ksrc = "qk%d" % src
                for sl in range(4):
                    wi = sl % 2
                    w1v = W01[wi][0:64, 0:8 * 256].rearrange("p (l n) -> p l n", l=8)
                    dma("sp", w1v, wb[wn][l, sl * 512:(sl + 1) * 512, :].rearrange("(l p) n -> p l n", p=64),
                        [K("wb", wn, l)], ["w01_%d" % wi])
                    for ll in range(8):
                        lidx = sl * 8 + ll
                        for mb in range(2):
                            mm(PS_P[mb][:, 0:127], w1v[:, ll, mb * 128:(mb + 1) * 128], srcT[0:64, lidx:lidx + 16 * 126 + 1:16],
                               lidx == 0, lidx == 31, ["w01_%d" % wi, ksrc], ["ps_p%d" % mb])
                            mm(PS_S[mb][:, 0:1], w1v[:, ll, mb * 128:(mb + 1) * 128], PETB[0:64, lidx:lidx + 1],
                               lidx == 0, lidx == 31, ["w01_%d" % wi, "petb"], ["ps_s%d" % mb])
                for mb in range(2):
                    cp("dve", HB[:, mb:mb + 1], PS_S[mb][:, 0:1], ["ps_s%d" % mb], ["hb"])
                    act(HG[kv][:, mb, 0:127], PS_P[mb][:, 0:127], AF.Gelu_apprx_tanh, ["ps_p%d" % mb, "hb"], ["hg%d" % kv],
                        bias=HB[:, mb:mb + 1])
            for mb in range(2):
                mm(PS_B[0:64, 0:127], W2T[:, 0, mb, :], HG[0][:, mb, 0:127], mb == 0, mb == 1, ["w2t", "hg0"], ["ps_b"])
            cp("act", KCT[0:64, 0:127], PS_B[0:64, 0:127], ["ps_b"], ["kct"])
            for mb in range(2):
                mm(PS_B[0:127, 128:192], HG[1][:, mb, 0:127], W2T[:, 1, mb, :], mb == 0, mb == 1, ["w2t", "hg1"], ["ps_b"])
            cp("act", VC[0:127, 0:64], PS_B[0:127, 128:192], ["ps_b"], ["vc"])
            cp("pool", VC[0:127, 128:192], VC[0:127, 0:64], ["vc"], ["vc"])

            def nsa_final(acc, ka, i, b, Gq, has_tiny):
                cs2 = slice(Gq * 512, (Gq + 1) * 512)
                ch, pb = ytile(10 + 3 * g + i)
                yv = YT[pb:pb + 64, ch, cs2]
                rb, krb = recip_rows(acc[64 - pb:128 - pb, :], ka, pb, 64 - pb)
                mm(PS_B[:, :], SELG[64:128, 3 * i + b, :], GSB[64:128, cs2], True, True, ["selg", "gsb"], ["ps_b"])
                tt("dve", T1[pb:pb + 64, :], PS_B[pb:pb + 64, :], rb, ALU.mult, ["ps_b", krb], ["t1"])
                if b == 0:
                    tt("dve", yv, acc[pb:pb + 64, :], T1[pb:pb + 64, :], ALU.mult, [ka, "t1"], ["yt"])
                else:
                    tt("dve", TMPB[b][pb:pb + 64, :], acc[pb:pb + 64, :], T1[pb:pb + 64, :], ALU.mult, [ka, "t1"], ["tmpb%d" % b])
                if b == 2:
                    tt("pool", SCR[pb:pb + 64, :], yv, TMPB[1][pb:pb + 64, :], ALU.add, ["yt", "tmpb1"], ["scr"])
                    tt("pool", yv, SCR[pb:pb + 64, :], TMPB[2][pb:pb + 64, :], ALU.add, ["scr", "tmpb2"], ["yt"])

            def vsel(slot, i):
                par = (10 + 3 * g + i) % 2
                return lambda kc0: (VV[:, kc0 // 128, slot, par * 64:par * 64 + 128], "vv")

            for i in range(3):
                for Gq in range(NTG):
                    cs2 = slice(Gq * 512, (Gq + 1) * 512)
                    steps = [(0, Gq * 512, 512, (CMASK[:, cs2], 0, "cmask"))]
                    s_before = st_ctr[0] % 3
                    par = (10 + 3 * g + i) % 2
                    acc, ka = attn_group(steps, QK[i], "qk%d" % i, KCT, "kct", lambda kc0, par=par: (VC[:, par * 64:par * 64 + 128], "vc"))
                    pt = PT[s_before]
                    kpt = "pt%d" % s_before
                    for qb in range(4):
                        mm(PS_P[0][:, qb * 33:qb * 33 + 33], pt[:, qb * 128:(qb + 1) * 128], OV33[:, :], True, True,
                           [kpt, "ov33"], ["ps_p0"])
                    imv = PS_P[0][:, 0:132].rearrange("p (q e) -> p q e", q=4)
                    ts("dve", RI[:, :], imv[:, :, 32], 1e-30, None, ALU.max, None, ["ps_p0"], ["ri"])
                    sc.op("dve", lambda e: e.reciprocal(out=RI[:, :], in_=RI[:, :]), ["ri"], ["ri"])
                    for qb in range(4):
                        t = Gq * 4 + qb
                        dsti = IMPS[:, t * 32:(t + 1) * 32]
                        if i == 0:
                            ts("dve", dsti, imv[:, qb, 0:32], RI[:, qb:qb + 1], None, ALU.mult, None, ["ps_p0", "ri"], ["imps"])
                        else:
                            stt(dsti, imv[:, qb, 0:32], RI[:, qb:qb + 1], dsti, ALU.mult, ALU.add, ["ps_p0", "ri", "imps"], ["imps"])
                    defer_fin(lambda acc=acc, ka=ka, i=i, Gq=Gq: nsa_final(acc, ka, i, 0, Gq, True))
            flush_fin()
            tt("dve", SCR[:, :], IMPS[:, :], M1[:, :], ALU.mult, ["imps", "m1"], ["scr"])
            tt("dve", SCR[:, :], SCR[:, :], M2[:, :], ALU.add, ["scr", "m2"], ["scr"])
            for t in range(NT):
                sc.op("dve", lambda e, t=t: e.max(out=MX[:, t, :], in_=SCR[:, t * 32:(t + 1) * 32]), ["scr"], ["mx"])
            for t in range(NT):
                ts("dve", SCR[:, t * 32:(t + 1) * 32], SCR[:, t * 32:(t + 1) * 32], MX[:, t, 7:8], None, ALU.is_ge, None, ["scr", "mx"], ["scr"])
            ts("dve", NSEL[:, :], SCR[:, :], -NEG, NEG, ALU.mult, ALU.add, ["scr"], ["nsel"])
            for tg in range(NTG):
                ppb = PS_P[tg % 2][:, :].bitcast(BF16)
                kp = "ps_p%d" % (tg % 2)
                for t4 in range(4):
                    t = tg * 4 + t4
                    sc.op("pe", lambda e, ppb=ppb, t=t, t4=t4: e.transpose(ppb[0:32, t4 * 128:(t4 + 1) * 128], NSEL[:, t * 32:(t + 1) * 32], IDENT[:, :]),
                          ["nsel", "ident"], [kp])
                cs_ = slice(tg * 512, (tg + 1) * 512)
                cp("act", QK[0][96:128, cs_], ppb[0:32, 0:512], [kp], ["qk0"])
                cp("pool", QK[1][96:128, cs_], QK[0][96:128, cs_], ["qk0"], ["qk1"])
                cp("pool", QK[2][96:128, cs_], QK[0][96:128, cs_], ["qk0"], ["qk2"])
            for i in range(3):
                for Gq in range(NTG):
                    steps = []
                    for j in range(4 * Gq + 4):
                        m = max(0, j - 4 * Gq)
                        q0 = (4 * Gq + m) * 128
                        nq = 512 - 128 * m
                        mask = (MASKAB[:, 0:128], 0, "maskab") if j >= 4 * Gq else None
                        steps.append((j * 128, q0, nq, mask))
                    acc, ka = attn_group(steps, QK[i], "qk%d" % i, QK[4], "qk4", vsel(0, i))
                    defer_fin(lambda acc=acc, ka=ka, i=i, Gq=Gq: nsa_final(acc, ka, i, 1, Gq, False))
                    steps = []
                    for j in range(max(0, 4 * Gq - 4), 4 * Gq + 4):
                        if j < 4 * Gq:
                            qhi = min(4 * Gq + 3, j + 4)
                            q0 = 4 * Gq * 128
                            nq = (qhi - 4 * Gq + 1) * 128
                            mask = (MASKAB[:, 256:384], nq - 128, "maskab") if qhi == j + 4 else None
                        else:
                            q0 = j * 128
                            nq = (4 * Gq + 4 - j) * 128
                            mask = (MASKAB[:, 0:128], 0, "maskab")
                        steps.append((j * 128, q0, nq, mask))
                    acc, ka = attn_group(steps, QK[i], "qk%d" % i, QK[5], "qk5", vsel(1, i))
                    defer_fin(lambda acc=acc, ka=ka, i=i, Gq=Gq: nsa_final(acc, ka, i, 2, Gq, False))
            flush_fin()


    RXM = [FA[:, 1536 + i * 512:2048 + i * 512] for i in range(3)]
    RXF = [FA[:, 2052 + i * 256:2308 + i * 256] for i in range(3)]
    rx_ctr = [0]
    YTF = YT.rearrange("p c t -> p (c t)")

    def merge_phase(l, xsrc):
        wp = WM[:, 0:8192].rearrange("p (c n) -> p c n", c=8)
        dma("sp", wp[:, 0:2, :], wb["w_pa"][l].rearrange("(c p) n -> p c n", p=128), [K("wb", "w_pa", l)], ["wm"])
        dma("sp", wp[:, 2:5, :], wb["w_pb"][l].rearrange("(c p) n -> p c n", p=128), [K("wb", "w_pb", l)], ["wm"])
        dma("sp", wp[:, 5:8, :], wb["w_pc"][l].rearrange("(c p) n -> p c n", p=128), [K("wb", "w_pc", l)], ["wm"])
        qkeys = ["qk%d" % i for i in range(8)]
        WO = QKALL[:, 0:8192].rearrange("p (c n) -> p c n", c=8)
        dma("sp", WO, wb["w_o"][l].rearrange("(c p) n -> p c n", p=128), [K("wb", "w_o", l)], qkeys)
        MGS = [QKALL[:, 8192 + i * 4096:12288 + i * 4096].rearrange("p (c t) -> p c t", c=8) for i in range(2)]
        sc.alias("mg0", ["qk4", "qk5"])
        sc.alias("mg1", ["qk6", "qk7"])
        for k_ in ("gsl0", "gsl1", "gsl2"):
            sc.alias(k_, ["w01_0", "w01_1"])
        fa_alias(["gt0", "gt1", "gt2", "rx0", "rx1", "rx2"])
        GT = [FA[:, b * 512:(b + 1) * 512] for b in range(3)]
        brch = [(0, 2), (2, 5), (5, 8)]
        brps = [(PS_A[0], "ps_a0"), (PS_A[1], "ps_a1"), (PS_B, "ps_b")]
        items = [(t4, half) for t4 in range(4) for half in range(2)]
        rxs = {}

        def ld(tg_, n_):
            t4_, half_ = items[n_]
            t_ = tg_ * 4 + t4_
            ri = rx_ctr[0] % 3
            rx_ctr[0] += 1
            rxs[(tg_, n_)] = (RXM[ri], "rx%d" % ri)
            dma("sp", RXM[ri], xsrc[t_ * 128:(t_ + 1) * 128, half_ * 512:(half_ + 1) * 512], xkeys(t_), ["rx%d" % ri])

        def wo_item(tg_, n_):
            MG, kmg = MGS[tg_ % 2], "mg%d" % (tg_ % 2)
            t4, half = items[n_]
            t = tg_ * 4 + t4
            if n_ == 0:
                ld(tg_, 0)
                ld(tg_, 1)
            pp = PS_P[half]
            kp = "ps_p%d" % half
            for c in range(8):
                mm(pp[:, :], MG[:, c, t4 * 128:(t4 + 1) * 128], WO[:, c, half * 512:(half + 1) * 512], c == 0, c == 7,
                   [kmg] + qkeys[0:4], [kp])
            if n_ + 2 < len(items):
                ld(tg_, n_ + 2)
            rx, krx = rxs[(tg_, n_)]
            tt("dve", rx, rx, pp[:, :], ALU.add, [krx, kp], [krx])
            dma("pool", xres[t * 128:(t + 1) * 128, half * 512:(half + 1) * 512], rx, [krx], [("xres", t, "h%d" % half)])

        for tg in range(NTG):
            hs, khs = next_hs(tg)
            cs_ = slice(tg * 512, (tg + 1) * 512)
            MG, kmg = MGS[tg % 2], "mg%d" % (tg % 2)
            for c in range(8):
                wi = gs_ctr[0] % 3
                gs_ctr[0] += 1
                gflat = W01ALL[:, wi * 3072:(wi + 1) * 3072]
                gsl = gflat.rearrange("p (b c n) -> p b c n", b=3, c=DC)
                dma("sp", gflat, wg[l, c], [K("wg", l)], ["gsl%d" % wi])
                for b in range(3):
                    for dc in range(DC):
                        mm(PS_S[b][:, :], gsl[:, b, dc, :], hs[:, dc, :], dc == 0, dc == DC - 1, ["gsl%d" % wi, khs], ["ps_s%d" % b])
                    k0, k1 = brch[b]
                    bp, kbp = brps[b]
                    for kc in range(k0, k1):
                        mm(bp[:, :], wp[:, kc, c * 128:(c + 1) * 128], YT[:, kc, cs_], kc == k0, kc == k1 - 1, ["wm", "yt"], [kbp])
                if tg >= 1:
                    wo_item(tg - 1, c)
                for b in range(3):
                    bp, kbp = brps[b]
                    act(GT[b], PS_S[b][:, :], AF.Sigmoid, ["ps_s%d" % b, "bcol"], ["gt%d" % b], bias=bias_ap("gt%d" % (b * 8 + c), 128))
                    tt("dve", GT[b], GT[b], bp[:, :], ALU.mult, ["gt%d" % b, kbp], ["gt%d" % b])
                tt("pool", GT[0], GT[0], GT[1], ALU.add, ["gt0", "gt1"], ["gt0"])
                tt("pool", MG[:, c, :], GT[0], GT[2], ALU.add, ["gt0", "gt2"], [kmg])
        for n_ in range(8):
            wo_item(NTG - 1, n_)

    def ffn_phase(l):
        memset("pool", AH[:, :, :], 0.0, ["ah"])
        sc.alias("yt2", ["yt"])
        for k_ in ("w01_0", "w01_1"):
            sc.alias(k_, ["gsl0", "gsl1", "gsl2"])
        fa_alias(["ab0", "ab1", "tb0", "tb1", "rx0", "rx1", "rx2"])
        for k_ in ("qk4", "qk5"):
            sc.alias(k_, ["mg0"])
        for k_ in ("qk6", "qk7"):
            sc.alias(k_, ["mg1"])
        qkeys = ["qk%d" % i for i in range(8)]
        ACTT = QKALL[:, 0:FC * 512].rearrange("p (c t) -> p c t", c=FC)
        AB = [FA[:, i * 514:(i + 1) * 514] for i in range(2)]
        TB = [FA[:, 1028 + i * 512:1028 + (i + 1) * 512] for i in range(2)]
        B6 = [(PS_S[0], "ps_s0"), (PS_S[1], "ps_s1"), (PS_S[2], "ps_s2"), (PS_A[0], "ps_a0"), (PS_A[1], "ps_a1"), (PS_B, "ps_b")]
        DW = [WM[:, 0:5632], YTF[:, 0:5632], YTF[:, 5632:11264]]
        dwk = ["wm", "yt", "yt2"]
        dw_ctr = 0
        cwv = CW[:, :].rearrange("p (c k) -> p c k", k=4)
        it = 0
        def load_dw(qd_):
            nonlocal dw_ctr
            di_ = dw_ctr % 3
            dw_ctr += 1
            dma("sp", DW[di_], wd[l, qd_], [K("wd", l)], [dwk[di_]])
            return di_

        for tg in range(NTG):
            hs, khs = next_hs(tg)
            dslots = [load_dw(0), load_dw(1)]
            slabs = {}

            def stP(fc, it_):
                fp, sub = divmod(fc, 2)
                if sub == 0:
                    wi = fp % 2
                    dma("sp", W01[wi][:, 0:4096], wu[l, fp], [K("wu", l)], ["w01_%d" % wi])
                    slabs[fp] = (W01[wi][:, 0:4096].rearrange("p (c b n) -> p c b n", c=DC, b=2), "w01_%d" % wi)
                usl, kw = slabs[fp]
                pa, kpa = B6[(2 * it_) % 6]
                pb_, kpb = B6[(2 * it_ + 1) % 6]
                for dc in range(DC):
                    mm(pa[:, :], usl[:, dc, 0, sub * 128:(sub + 1) * 128], hs[:, dc, :], dc == 0, dc == DC - 1, [kw, khs], [kpa])
                for dc in range(DC):
                    mm(pb_[:, :], usl[:, dc, 1, sub * 128:(sub + 1) * 128], hs[:, dc, :], dc == 0, dc == DC - 1, [kw, khs], [kpb])

            def stE1(fc, it_):
                pa, kpa = B6[(2 * it_) % 6]
                A, kA = AB[it_ % 2], "ab%d" % (it_ % 2)
                T, kT = TB[it_ % 2], "tb%d" % (it_ % 2)
                cp("pool", A[:, 0:2], AH[:, fc, :], ["ah"], [kA])
                cp("act", A[:, 2:514], pa[:, :], [kpa], [kA])
                cp("pool", AH[:, fc, :], A[:, 512:514], [kA], ["ah"])
                ts("dve", T, A[:, 0:512], cwv[:, fc, 0:1], cwv[:, fc, 3:4], ALU.mult, ALU.add, [kA, "cw"], [kT])
                stt(T, A[:, 1:513], cwv[:, fc, 1:2], T, ALU.mult, ALU.add, [kA, "cw", kT], [kT], relaxed=True)
                stt(T, A[:, 2:514], cwv[:, fc, 2:3], T, ALU.mult, ALU.add, [kA, "cw", kT], [kT], relaxed=True)

            def stE2(fc, it_):
                pb_, kpb = B6[(2 * it_ + 1) % 6]
                T, kT = TB[it_ % 2], "tb%d" % (it_ % 2)
                act(T, T, AF.Gelu_apprx_tanh, [kT], [kT])
                tt("dve", ACTT[:, fc, :], T, pb_[:, :], ALU.mult, [kT, kpb], qkeys[0:6])

            for fc in range(FC):
                stP(fc, it + fc)
                stE1(fc, it + fc)
                if fc >= 1:
                    stE2(fc - 1, it + fc - 1)
            stE2(FC - 1, it + FC - 1)
            it += FC
            accs = [(PS_P[0], "ps_p0"), (PS_P[1], "ps_p1"), (PS_P[0], "ps_p0"), (PS_P[1], "ps_p1")]
            ditems = [(qd, t4) for qd in range(4) for t4 in range(4)]
            rxs = {}

            def ldf(n_):
                qd_, t4_ = ditems[n_]
                t_ = tg * 4 + t4_
                ri = rx_ctr[0] % 3
                rx_ctr[0] += 1
                rxs[n_] = (RXF[ri], "rx%d" % ri)
                dma("sp", RXF[ri], xres[t_ * 128:(t_ + 1) * 128, qd_ * 256:(qd_ + 1) * 256], xkeys(t_), ["rx%d" % ri])

            ldf(0)
            ldf(1)
            for n_, (qd, t4) in enumerate(ditems):
                if t4 == 0 and qd + 2 < 4:
                    dslots.append(load_dw(qd + 2))
                di = dslots[qd]
                dsl = DW[di].rearrange("p (c n) -> p c n", c=FC)
                t = tg * 4 + t4
                pp, kp = accs[t4]
                for fc in range(FC):
                    mm(pp[:, 0:256], ACTT[:, fc, t4 * 128:(t4 + 1) * 128], dsl[:, fc, :], fc == 0, fc == FC - 1, qkeys[0:6] + [dwk[di]], [kp])
                if n_ + 2 < len(ditems):
                    ldf(n_ + 2)
                rx, krx = rxs[n_]
                tt("dve", rx, rx, pp[:, 0:256], ALU.add, [krx, kp], [krx])
                dma("pool", xres[t * 128:(t + 1) * 128, qd * 256:(qd + 1) * 256], rx, [krx], [("xres", t, "q%d" % qd)])

    def driver():
        for s_ in range(nseq):
            for l in range(depth):
                load_layer_small(l)
                sc.alias("yt", ["yt2"])
                for k_ in ("w01_0", "w01_1"):
                    sc.alias(k_, ["gsl0", "gsl1", "gsl2"])
                for k_ in ("qk4", "qk5"):
                    sc.alias(k_, ["mg0"])
                for k_ in ("qk6", "qk7"):
                    sc.alias(k_, ["mg1"])
                xsrc = x_in[s_] if l == 0 else xres
                norm_phase(xsrc, 2 * l)
                if stop_after == "norm1":
                    return ["hTd"]
                if "fox" in enable:
                    fox_phase(l)
                else:
                    memset("pool", YTF[:, 0:2 * S], 0.0, ["yt"])
                if s_ == 0 and l == 0:
                    relayout(0)
                if "dil" in enable:
                    dil_phase(l)
                else:
                    memset("pool", YTF[:, 2 * S:5 * S], 0.0, ["yt"])
                if "nsa" in enable:
                    nsa_phase(l)
                else:
                    memset("pool", YTF[:, 5 * S:8 * S], 0.0, ["yt"])
                if s_ == 0 and l == 0:
                    for l2 in range(1, depth):
                        relayout(l2)
                if stop_after == "mix":
                    dma("sp", ytd, YT, ["yt"], ["ytd"])
                    return ["ytd"]
                merge_phase(l, xsrc)
                if stop_after == "merge":
                    return [k_ for t_ in range(NT) for k_ in xkeys(t_)]
                norm_phase(xres, 2 * l + 1)
                if stop_after == "norm2":
                    return ["hTd"]
                ffn_phase(l)
                if stop_after == "ffn":
                    return [k_ for t_ in range(NT) for k_ in xkeys(t_)]
            norm_phase(xres, 2 * depth, final_out=out[s_])
        return ["outd"]

    fkeys = driver()
    sc.final_wait("sp", fkeys)
    global LAST_SCHED
    LAST_SCHED = sc
    sc.emit()
    es.close()
    return nc


def _host_inputs(inp, depth):
    h = {}
    g = [inp["norm1_g"], inp["norm2_g"]]
    ngb = np.zeros((2 * depth + 1, 128, D), np.float32)
    for l in range(depth):
        ngb[2 * l] = np.broadcast_to(g[0][l], (128, D))
        ngb[2 * l + 1] = np.broadcast_to(g[1][l], (128, D))
    ngb[2 * depth] = np.broadcast_to(inp["final_g"], (128, D))
    h["ngb"] = ngb
    gl = [ngb[i, 0] for i in range(2 * depth + 1)]
    h["ngt"] = np.stack([np.repeat(g_.reshape(DC, 128).T[:, :, None], 128, axis=2).reshape(128, D) for g_ in gl]).astype(np.float32)
    bcol = np.zeros((depth, 128, NB), np.float32)
    for l in range(depth):
        for i, (n, segs) in enumerate(BIAS_BLOCKS):
            for (p0, c0, m) in segs:
                bcol[l, p0:p0 + m, i] = inp["b_in"][l, c0:c0 + m]
    h["bcol"] = bcol
    h["vrow"] = np.ascontiguousarray(inp["b_in"][:depth][:, None, VROW_COLS]).astype(np.float32)
    cw = np.zeros((depth, 128, FC, 4), np.float32)
    for l in range(depth):
        for k in range(3):
            cw[l, :, :, k] = inp["conv_w"][l, k].reshape(FC, 128).T
        cw[l, :, :, 3] = inp["conv_b"][l].reshape(FC, 128).T
    h["cw"] = cw.reshape(depth, 128, FC * 4)
    h["pet"] = np.ascontiguousarray(np.transpose(inp["cmp_pe"][:depth], (0, 2, 1))).astype(np.float32)
    return h


_CACHE = {}


def kernel(**inputs):
    ncores = 8
    depth = inputs["w_in"].shape[0]
    B = inputs["x"].shape[0]
    nseq = B // ncores
    key = (nseq, depth)
    if key not in _CACHE:
        _CACHE[key] = build_program(nseq, depth)
    nc = _CACHE[key]
    consts = _make_consts()
    hi = _host_inputs(inputs, depth)
    common = {}
    for n, _ in WEIGHTS:
        common[n] = np.ascontiguousarray(inputs[n], dtype=np.float32)
    common.update(hi)
    common.update(consts)
    x = np.ascontiguousarray(inputs["x"], dtype=np.float32)
    in_maps = []
    for c in range(ncores):
        m = dict(common)
        m["x"] = x[c * nseq:(c + 1) * nseq]
        in_maps.append(m)
    res = run_bass_kernel_spmd(nc, in_maps, core_ids=list(range(ncores)))
    return np.concatenate([r["out"] for r in res.results], axis=0)
```
